# Optimizing a Trainium2 kernel written in Bass

```python
import jax, jax.numpy as jnp
from jax import lax
import numpy as np

D_MODEL = 2048
BATCH = 32
SEQ = 256
DEPTH = 2
DEC_BATCH = 4
DEC_SEQ = 4096
PAST_LEN = 256

GRID_W = 64
H_A = 8
DK = 128
DV = 128
W_A = H_A * DK
N_FG = 4
FG = 256
W_B = N_FG * FG
CHUNK = 32
D_FF = -(-8 * D_MODEL // (3 * 256)) * 256
N_IN = 5 * W_A + W_B + 2 * D_MODEL
EPS = 1e-6
POS_BASE = 10000.0

kernel_name = "hybrid_hgrn2_fnet_diffusion_step"


def rmsnorm(x, w):
    xf = x.astype(jnp.float32)
    y = xf * lax.rsqrt(jnp.mean(xf * xf, axis=-1, keepdims=True) + EPS)
    return (y * w.astype(jnp.float32)).astype(x.dtype)


def adaln(cvec, w_mod_l, b_mod_l):
    m = (jax.nn.silu(cvec) @ w_mod_l + b_mod_l).reshape(cvec.shape[0], 1, 6, D_MODEL)
    return tuple(m[:, :, k] for k in range(6))


def grid_pos_embed(n, dtype):
    rows = n // GRID_W
    r, col = jnp.meshgrid(jnp.arange(rows, dtype=jnp.float32), jnp.arange(GRID_W, dtype=jnp.float32), indexing="ij")
    quarter = D_MODEL // 4
    omega = 1.0 / (POS_BASE ** (jnp.arange(quarter, dtype=jnp.float32) / quarter))
    def enc(p):
        a = p.reshape(-1)[:, None] * omega[None, :]
        return jnp.concatenate([jnp.sin(a), jnp.cos(a)], axis=-1)
    return jnp.concatenate([enc(r), enc(col)], axis=-1).astype(dtype)


def chunk_scan(q, k, v, logf, s0):
    bsz, n, h, _ = q.shape
    nc = n // CHUNK
    def to_chunks(t):
        return t.reshape(bsz, nc, CHUNK, h, t.shape[-1]).transpose(1, 0, 3, 2, 4)
    tri = jnp.tril(jnp.ones((CHUNK, CHUNK), dtype=bool))
    def step(S, xs):
        qc, kc, vc, gc = xs
        b = jnp.cumsum(gc, axis=-2)
        rel = jnp.where(tri[:, :, None], b[..., :, None, :] - b[..., None, :, :], -jnp.inf)
        a = jnp.einsum('bhtd,bhsd,bhtsd->bhts', qc, kc, jnp.exp(rel))
        o = jnp.einsum('bhts,bhsv->bhtv', a, vc) + jnp.einsum('bhtd,bhdv->bhtv', qc * jnp.exp(b), S)
        b_last = b[..., -1:, :]
        S = jnp.exp(b_last[..., 0, :])[..., None] * S + jnp.einsum('bhsd,bhsv->bhdv', kc * jnp.exp(b_last - b), vc)
        return S, o
    S, o = lax.scan(step, s0, (to_chunks(q), to_chunks(k), to_chunks(v), to_chunks(logf)))
    o = o.transpose(1, 0, 3, 2, 4).reshape(bsz, n, h, v.shape[-1])
    return o, S


def mixer(h, w_in_l, lb_l, g_norm_l, w_a_l, w_b_l, w_out_l, s0f, s0b):
    bsz, n, _ = h.shape
    p = h @ w_in_l
    q, ff, fb, iv, og, u, ga, gb = jnp.split(
        p, [W_A, 2 * W_A, 3 * W_A, 4 * W_A, 5 * W_A, 5 * W_A + W_B, 5 * W_A + W_B + D_MODEL], axis=-1)
    def heads(t):
        return t.astype(jnp.float32).reshape(bsz, n, H_A, -1)
    qh, vh = heads(q), heads(iv)
    def gates(fr, lbd):
        lbh = lbd.reshape(H_A, DK)
        f = lbh + (1.0 - lbh) * jax.nn.sigmoid(heads(fr))
        return 1.0 - f, jnp.log(f)
    kf, gf = gates(ff, lb_l[0])
    kb, gbw = gates(fb, lb_l[1])
    of, sf = chunk_scan(qh, kf, vh, gf, s0f)
    flip = lambda t: jnp.flip(t, axis=1)
    ob, sb = chunk_scan(flip(qh), flip(kb), flip(vh), flip(gbw), s0b)
    o = of + flip(ob)
    o = o * lax.rsqrt(jnp.mean(o * o, axis=-1, keepdims=True) + EPS) * g_norm_l.astype(jnp.float32)
    o = o.reshape(bsz, n, W_A).astype(h.dtype) * jax.nn.silu(og)
    uf = u.astype(jnp.float32).reshape(bsz, n, N_FG, FG)
    z = jnp.fft.fft2(uf, axes=(1, 3), norm="ortho").real.reshape(bsz, n, W_B).astype(h.dtype)
    merged = jax.nn.sigmoid(ga) * (o @ w_a_l) + jax.nn.sigmoid(gb) * (z @ w_b_l)
    return merged @ w_out_l, sf, sb


def layer(x, mod, n1, n2, w_in_l, lb_l, g_norm_l, w_a_l, w_b_l, w_out_l, w_ff_in_l, w_ff_out_l, s0f, s0b):
    sh1, sc1, g1, sh2, sc2, g2 = mod
    h = rmsnorm(x, n1) * (1.0 + sc1) + sh1
    m, sf, sb = mixer(h, w_in_l, lb_l, g_norm_l, w_a_l, w_b_l, w_out_l, s0f, s0b)
    x = x + g1 * m
    h = rmsnorm(x, n2) * (1.0 + sc2) + sh2
    a, gt = jnp.split(h @ w_ff_in_l, 2, axis=-1)
    x = x + g2 * ((jax.nn.silu(a) * gt) @ w_ff_out_l)
    return x, sf, sb


def setup_inputs(seed: int = 0) -> dict:
    key = jax.random.key(seed)
    ks = jax.random.split(key, 20)
    nrm = lambda k, shape, s: jax.random.normal(k, shape, jnp.float32) * s
    return {
        "x_prompt": nrm(ks[0], (BATCH, SEQ, D_MODEL), 1.0),
        "x_sample": nrm(ks[1], (DEC_BATCH, DEC_SEQ, D_MODEL), 1.0),
        "state_hgrn": nrm(ks[2], (DEC_BATCH, DEPTH, 2, H_A, DK, DV), 1.0),
        "c": nrm(ks[3], (DEC_BATCH, D_MODEL), 1.0),
        "c_ctx": nrm(ks[4], (D_MODEL,), 1.0),
        "w_mod": nrm(ks[5], (DEPTH, D_MODEL, 6 * D_MODEL), D_MODEL ** -0.5),
        "b_mod": nrm(ks[6], (DEPTH, 6 * D_MODEL), 0.02),
        "norm_mix": 1.0 + nrm(ks[7], (DEPTH, D_MODEL), 0.02),
        "norm_ffn": 1.0 + nrm(ks[8], (DEPTH, D_MODEL), 0.02),
        "w_in": nrm(ks[9], (DEPTH, D_MODEL, N_IN), D_MODEL ** -0.5),
        "lb_raw": nrm(ks[10], (DEPTH, 2, W_A), 1.0),
        "g_norm": 1.0 + nrm(ks[11], (DEPTH, DV), 0.02),
        "w_a": nrm(ks[12], (DEPTH, W_A, D_MODEL), W_A ** -0.5),
        "w_b": nrm(ks[13], (DEPTH, W_B, D_MODEL), W_B ** -0.5),
        "w_out": nrm(ks[14], (DEPTH, D_MODEL, D_MODEL), D_MODEL ** -0.5),
        "w_ff_in": nrm(ks[15], (DEPTH, D_MODEL, 2 * D_FF), D_MODEL ** -0.5),
        "w_ff_out": nrm(ks[16], (DEPTH, D_FF, D_MODEL), D_FF ** -0.5),
        "norm_final": 1.0 + nrm(ks[17], (D_MODEL,), 0.02),
    }


def reference(x_prompt, x_sample, state_hgrn, c, c_ctx, w_mod, b_mod, norm_mix, norm_ffn, w_in, lb_raw,
              g_norm, w_a, w_b, w_out, w_ff_in, w_ff_out, norm_final):
    lb_all = jnp.cumsum(jax.nn.softmax(lb_raw.astype(jnp.float32), axis=0), axis=0)
    lb_all = lb_all - lb_all[0:1]
    xc = x_prompt
    xs = x_sample + grid_pos_embed(x_sample.shape[1], x_sample.dtype)[None]
    zeros = jnp.zeros((x_prompt.shape[0], H_A, DK, DV), jnp.float32)
    new_states = []
    for l in range(DEPTH):
        mod_ctx = adaln(c_ctx[None, :], w_mod[l], b_mod[l])
        mod_lat = adaln(c, w_mod[l], b_mod[l])
        xc, sf, sb = layer(xc, mod_ctx, norm_mix[l], norm_ffn[l], w_in[l], lb_all[l], g_norm[l], w_a[l], w_b[l],
                           w_out[l], w_ff_in[l], w_ff_out[l], zeros, zeros)
        new_states.append(jnp.stack([sf, sb], axis=1))
        xs, _, _ = layer(xs, mod_lat, norm_mix[l], norm_ffn[l], w_in[l], lb_all[l], g_norm[l], w_a[l], w_b[l],
                         w_out[l], w_ff_in[l], w_ff_out[l],
                         state_hgrn[:, l, 0].astype(jnp.float32), state_hgrn[:, l, 1].astype(jnp.float32))
    state_hgrn_new = jnp.stack(new_states, axis=1).astype(x_prompt.dtype)
    y_prompt = rmsnorm(xc, norm_final)
    y_sample = rmsnorm(xs, norm_final)
    return (y_prompt, y_sample, state_hgrn_new)
```

```python
import contextlib
import numpy as np
import ml_dtypes
import concourse.bass as bass
import concourse.mybir as mybir
from concourse.bass_utils import run_bass_kernel_spmd

F32 = mybir.dt.float32
BF16 = mybir.dt.bfloat16
AF = mybir.ActivationFunctionType
ALU = mybir.AluOpType
AX = mybir.AxisListType

D = 2048
KC = 16
H_A = 8
W_A = 1024
W_B = 1024
N_IN = 10240
D_FF = 5632
FC = 44
EPS = 1e-6
CH = 32
DEPTH = 2
SEG = 256


class Tl:
    __slots__ = ("name", "w", "r", "s", "pr")

    def __init__(self, name):
        self.name = name
        self.w = {}
        self.r = {}
        self.pr = {}
        self.s = {}


def _addref(d, ref):
    if ref[0] == "e":
        if d.get(ref[1], -1) < ref[2]:
            d[ref[1]] = ref[2]
    else:
        d[id(ref[1])] = ref[1]


class Op:
    __slots__ = ("eng", "idx", "fn", "edeps", "ddeps", "sig", "dma", "val")

    def __init__(self, eng, idx, fn):
        self.eng = eng
        self.idx = idx
        self.fn = fn
        self.edeps = {}
        self.ddeps = {}
        self.sig = False
        self.dma = None
        self.val = 0


COMPUTE = ("pe", "act", "dve", "pool")
QUEUES = ("sp", "act", "pool")


class Prog:
    def __init__(self, nc, stack):
        self.nc = nc
        self.stack = stack
        self.ops = {e: [] for e in ("pe", "act", "dve", "pool", "sp")}
        self.known_e = {e: {} for e in self.ops}
        self.known_d = {e: {} for e in self.ops}
        self.dsems = []
        self.free = {"hw": [], "sw": []}
        self.allsems = {}
        self.nsem = 0
        self._cap = None

    def capture(self):
        self._cap = []

    def end_capture(self):
        c, self._cap = self._cap, None
        return c

    def replay(self, lists):
        n = max(len(x) for x in lists)
        for i in range(n):
            for x in lists:
                if i < len(x):
                    kind, args, kw = x[i]
                    (self.op if kind == "op" else self.dma)(*args, **kw)

    def tile(self, name):
        return Tl(name)

    def _dsem(self, t, kind):
        if kind not in t.s:
            if self.free[kind]:
                sem, base = self.free[kind].pop()
            else:
                sem, base = self.stack.enter_context(self.nc.semaphore("d%d" % self.nsem)), 0
                self.nsem += 1
            t.s[kind] = [sem, base]
            for e in self.known_d:
                self.known_d[e][(id(t), kind)] = base
            if t not in self.dsems:
                self.dsems.append(t)
        return t.s[kind]

    def _collect(self, op, reads, writes, merge):
        eng = op.eng

        def add(refs, skip_same):
            for k, v in refs.items():
                if isinstance(k, str):
                    e2, i2 = k, v
                    if e2 == eng and (skip_same or eng == "pe" or eng == "sp"):
                        continue
                    if self.known_e[eng].get(e2, -1) >= i2:
                        continue
                    if op.edeps.get(e2, -1) < i2:
                        op.edeps[e2] = i2
                else:
                    t = v
                    for kind, (sem_, c) in t.s.items():
                        if self.known_d[eng].get((id(t), kind), 0) >= c:
                            continue
                        op.ddeps[(id(t), kind)] = (sem_, c)

        for t in reads:
            add(t.w, False)
        for t in writes:
            if not merge:
                add(t.w, True)
            else:
                add(t.pr, True)
            add(t.r, True)
        for e2, i2 in op.edeps.items():
            self.known_e[eng][e2] = i2
        for k, (sem_, v) in op.ddeps.items():
            self.known_d[eng][k] = v

    def op(self, eng, fn, reads=(), writes=(), merge=False):
        if self._cap is not None:
            self._cap.append(("op", (eng, fn), dict(reads=reads, writes=writes, merge=merge)))
            return None
        op = Op(eng, len(self.ops[eng]), fn)
        self._collect(op, reads, writes, merge)
        self.ops[eng].append(op)
        ref = ("e", eng, op.idx)
        for t in reads:
            _addref(t.r, ref)
        for t in writes:
            if not merge:
                t.w = {}
                t.pr = t.r
                t.r = {}
            _addref(t.w, ref)
        return op

    def dma(self, q, fn, semtile, reads=(), writes=(), merge=False):
        if self._cap is not None:
            self._cap.append(("dma", (q, fn, semtile), dict(reads=reads, writes=writes, merge=merge)))
            return None
        op = Op(q, len(self.ops[q]), fn)
        self._collect(op, reads, writes, merge)
        rec = self._dsem(semtile, "sw" if q == "pool" else "hw")
        rec[1] += 16
        op.dma = (rec[0], semtile)
        self.allsems[id(rec[0])] = [rec[0], rec[1]]
        self.ops[q].append(op)
        ref = ("d", semtile)
        for t in reads:
            _addref(t.r, ref)
        for t in writes:
            if not merge:
                t.w = {}
                t.pr = t.r
                t.r = {}
            _addref(t.w, ref)
        return op

    def barrier(self):
        last = {}
        for e in COMPUTE:
            i = len(self.ops[e]) - 1
            while i >= 0 and (self.ops[e][i].fn is None or self.ops[e][i].dma is not None):
                i -= 1
            last[e] = i
        for e in self.ops:
            op = Op(e, len(self.ops[e]), None)
            for e2, i2 in last.items():
                if e2 != e and i2 >= 0 and self.known_e[e].get(e2, -1) < i2:
                    op.edeps[e2] = i2
                    self.known_e[e][e2] = i2
            for t in self.dsems:
                for kind, (sem_, c) in t.s.items():
                    if self.known_d[e].get((id(t), kind), 0) < c:
                        op.ddeps[(id(t), kind)] = (sem_, c)
                        self.known_d[e][(id(t), kind)] = c
            self.ops[e].append(op)
        for t in self.dsems:
            for kind, (sem_, c) in t.s.items():
                self.free[kind].append((sem_, c))
            t.s = {}
        self.dsems = []

    def emit(self):
        nc = self.nc
        esem = {e: self.stack.enter_context(nc.semaphore("e_" + e)) for e in COMPUTE}
        for e in self.ops:
            for op in self.ops[e]:
                for e2, i2 in op.edeps.items():
                    self.ops[e2][i2].sig = True
        for e in COMPUTE:
            c = 0
            for op in self.ops[e]:
                if op.sig:
                    assert op.fn is not None and op.dma is None, "bad signalling op"
                    c += 1
                op.val = c
        final = list(self.allsems.values())

        def run(ename, eng):
            for op in self.ops[ename]:
                for e2, i2 in op.edeps.items():
                    eng.wait_ge(esem[e2], self.ops[e2][i2].val)
                for k, (sem_, v) in op.ddeps.items():
                    eng.wait_ge(sem_, v)
                if op.fn is None:
                    continue
                ins = op.fn(eng)
                if op.dma is not None:
                    ins.then_inc(op.dma[0], 16)
                elif op.sig:
                    ins.then_inc(esem[ename], 1)
            if ename == "sp":
                for sem_, v in final:
                    eng.wait_ge(sem_, v)

        with nc.Block() as block:
            @block.sync
            def _(e):
                run("sp", e)

            @block.scalar
            def _(e):
                run("act", e)

            @block.vector
            def _(e):
                run("dve", e)

            @block.gpsimd
            def _(e):
                run("pool", e)

            @block.tensor
            def _(e):
                run("pe", e)


class Sb:
    def __init__(self, big, nwords):
        self.big = big
        self.n = nwords
        self.off = 0

    def reset(self, off=0):
        self.off = off

    def f32(self, cols):
        a = self.off
        self.off += cols
        assert self.off <= self.n, ("SBUF overflow", self.off, self.n)
        return self.big[:, a:a + cols]

    def bf16(self, cols):
        w = (cols + 1) // 2
        a = self.off
        self.off += w
        assert self.off <= self.n, ("SBUF overflow", self.off, self.n)
        return self.big[:, a:a + w].bitcast(BF16)


def build(NSEG, debug=False):
    NT = NSEG * SEG
    NTT = NT // 128
    TB = min(1024, NT)
    NB = NT // TB
    TBT = TB // 128
    NH5 = TB // 512
    NCHK = NT // CH

    import os
    nc = bass.Bass("TRN2", target_bir_lowering=False)
    stack = contextlib.ExitStack()
    P = Prog(nc, stack)

    def din(name, shape, dt=F32):
        return nc.dram_tensor(name, list(shape), dt, kind="ExternalInput").ap()

    def dscr(name, shape, dt=F32, out=False):
        kind = "ExternalOutput" if (out or debug) else "Internal"
        return nc.dram_tensor(name, list(shape), dt, kind=kind).ap()

    xin = din("xin", [NT, D])
    pe_in = din("pe", [NT, D])
    cvec = din("cvec", [D])
    s0 = din("s0", [DEPTH, 2, H_A, 128, 128])
    carry_in = din("carry", [128, 1])
    dn = din("dn", [2, NT, NT], BF16)
    cfsf = din("cfsf", [256, 512], BF16)
    masks = din("masks", [4, 128, 128])
    cmask = din("cmask", [128, 128])
    w_mod = din("w_mod", [DEPTH, D, 6 * D])
    b_mod = din("b_mod", [DEPTH, 6 * D])
    norm_mix = din("norm_mix", [DEPTH, D])
    norm_ffn = din("norm_ffn", [DEPTH, D])
    w_in = din("w_in", [DEPTH, D, N_IN])
    lb_raw = din("lb_raw", [DEPTH, 2, W_A])
    g_norm = din("g_norm", [DEPTH, 128])
    w_a = din("w_a", [DEPTH, W_A, D])
    w_b = din("w_b", [DEPTH, W_B, D])
    w_out = din("w_out", [DEPTH, D, D])
    w_ff_in = din("w_ff_in", [DEPTH, D, 2 * D_FF])
    w_ff_out = din("w_ff_out", [DEPTH, D_FF, D])
    norm_final = din("norm_final", [D])
    y_out = nc.dram_tensor("y", [NT, D], F32, kind="ExternalOutput").ap()
    st_out = nc.dram_tensor("st", [DEPTH, NSEG, 2, H_A, 128, 128], F32, kind="ExternalOutput").ap()
    wb_in = dscr("wb_in", [DEPTH, D, N_IN], BF16)
    wb_a = dscr("wb_a", [DEPTH, W_A, D], BF16)
    wb_b = dscr("wb_b", [DEPTH, W_B, D], BF16)
    wb_out = dscr("wb_out", [DEPTH, D, D], BF16)
    wb_ffi = dscr("wb_ffi", [DEPTH, D, 2 * D_FF], BF16)
    wb_ffo = dscr("wb_ffo", [DEPTH, D_FF, D], BF16)
    xs = dscr("xs", [NT, D])
    pq = dscr("pq", [24, 128, NT])
    pb = dscr("pb", [48, 128, NT], BF16)
    vv = dscr("vv", [NT, W_A], BF16)
    og_s = dscr("ogs", [8, 128, NT], BF16)
    z_s = dscr("zs", [8, 128, NT], BF16)
    g_s = dscr("gs", [DEPTH, 2, D])

    big_h = stack.enter_context(nc.sbuf_tensor("big", [128, 49152], F32))
    SB = Sb(big_h, 49152)
    banks = [stack.enter_context(nc.psum_tensor("ps%d" % i, [128, 512], F32)) for i in range(8)]
    bank_t = [P.tile("bank%d" % i) for i in range(8)]

    c_ident = SB.f32(128)
    c_mf = SB.f32(128)
    c_mb = SB.f32(128)
    c_cm = SB.f32(128)
    c_identb = SB.bf16(128)
    c_ones = SB.f32(128)
    c_rm = SB.f32(128)
    c_carry = SB.f32(1)
    c_eps = SB.f32(1)
    v_sc1 = [SB.f32(KC) for _ in range(DEPTH)]
    v_sh1 = [SB.f32(KC) for _ in range(DEPTH)]
    v_sc2 = [SB.f32(KC) for _ in range(DEPTH)]
    v_sh2 = [SB.f32(KC) for _ in range(DEPTH)]
    v_lb = SB.f32(16)
    v_oml = SB.f32(16)
    v_gn = SB.f32(DEPTH)
    T_const = P.tile("const")
    T_vec = P.tile("vecs")
    PERSIST = SB.off

    def ld(q, dst, src, tl, reads=(), extra_w=()):
        P.dma(q, lambda e: e.dma_start(out=dst, in_=src), tl, reads=reads, writes=(tl,) + tuple(extra_w))

    for i, (dst, src) in enumerate([(c_ident, masks[2]), (c_mf, masks[0]), (c_mb, masks[1]), (c_cm, cmask), (c_rm, masks[3]),
                                    (c_carry, carry_in)]):
        P.dma("sp", (lambda d_, s_: (lambda e: e.dma_start(out=d_, in_=s_)))(dst, src), T_const,
              writes=(T_const,), merge=True)
    P.op("dve", lambda e: e.tensor_copy(c_identb, c_ident), reads=(T_const,), writes=(T_const,), merge=True)
    P.op("dve", lambda e: e.memset(c_ones, 1.0), writes=(T_const,), merge=True)
    P.op("dve", lambda e: e.memset(c_eps, EPS), writes=(T_const,), merge=True)

    T_w = {}
    NST = 4

    def conv_gen(l, stg_f, stg_b, T_sf, T_sb, ldq, stq, engs):
        tiles = []
        for src, dst, rows, cols, key in ((w_in[l], wb_in[l], D, N_IN, ("in", l)), (w_a[l], wb_a[l], W_A, D, ("a", l)),
                                          (w_b[l], wb_b[l], W_B, D, ("b", l)), (w_out[l], wb_out[l], D, D, ("out", l)),
                                          (w_ff_in[l], wb_ffi[l], D, 2 * D_FF, ("ffi", l)),
                                          (w_ff_out[l], wb_ffo[l], D_FF, D, ("ffo", l))):
            T_w[key] = P.tile("w_" + str(key))
            for r0 in range(0, rows, 128):
                for c0 in range(0, cols, 2048):
                    cw = min(2048, cols - c0)
                    tiles.append((src[r0:r0 + 128, c0:c0 + cw], dst[r0:r0 + 128, c0:c0 + cw], cw, key))
        LA = 2

        def load(i):
            s_ap, d_ap, cw, key = tiles[i]
            b = i % NST
            sf = stg_f[b][:, :cw]
            P.dma(ldq, lambda e: e.dma_start(out=sf, in_=s_ap), T_sf[b], writes=(T_sf[b],))

        def finish(i):
            s_ap, d_ap, cw, key = tiles[i]
            b = i % NST
            sf, sb_ = stg_f[b][:, :cw], stg_b[b][:, :cw]
            ce = engs[i % len(engs)]
            if ce == "act":
                P.op("act", lambda e: e.copy(sb_, sf), reads=(T_sf[b],), writes=(T_sb[b],))
            else:
                P.op(ce, lambda e: e.tensor_copy(sb_, sf), reads=(T_sf[b],), writes=(T_sb[b],))
            P.dma(stq, lambda e: e.dma_start(out=d_ap, in_=sb_), T_sb[b], reads=(T_sb[b],), writes=(T_w[key],), merge=True)

        for i in range(len(tiles) + LA):
            if i < len(tiles):
                load(i)
            if i >= LA:
                finish(i - LA)
                yield

    SB.reset(PERSIST)
    stg_f0 = [SB.f32(2048) for _ in range(NST)]
    stg_b0 = [SB.bf16(2048) for _ in range(NST)]
    T_sf0 = [P.tile("sf%d" % i) for i in range(NST)]
    T_sb0 = [P.tile("sb%d" % i) for i in range(NST)]
    BGCONV = os.environ.get("BGCONV", "1") == "1"
    for l_ in range(1 if BGCONV else DEPTH):
        for _ in conv_gen(l_, stg_f0, stg_b0, T_sf0, T_sb0, "sp", "pool", ["dve", "act", "pool"]):
            pass
    P.barrier()

    SB.reset(PERSIST)
    m_ccol = SB.f32(KC)
    m_scol = SB.f32(KC)
    m_rep = SB.f32(KC * 128)
    m_w = [SB.f32(KC * 512) for _ in range(2)]
    m_b = [SB.f32(512) for _ in range(2)]
    m_row = SB.f32(512)
    m_tmp = SB.f32(128)
    m_lb = SB.f32(32)
    m_nm = SB.f32(2 * KC)
    m_nf = SB.f32(2 * KC)
    T_mc = P.tile("mc")
    T_mw = [P.tile("mw%d" % i) for i in range(2)]
    T_mb = [P.tile("mb%d" % i) for i in range(2)]
    T_mrow = P.tile("mrow")
    T_mtmp = P.tile("mtmp")
    T_mlb = P.tile("mlb")
    T_mn = P.tile("mn")
    T_gs = P.tile("gs")

    P.dma("sp", lambda e: e.dma_start(out=m_ccol, in_=cvec.rearrange("(k p) -> p k", p=128),
                                      allow_slow_non_contiguous=True), T_mc, writes=(T_mc,))
    P.op("act", lambda e: e.activation(out=m_scol, in_=m_ccol, func=AF.Silu), reads=(T_mc,), writes=(T_mc,))
    m_rep3 = m_rep.rearrange("p (k m) -> p k m", m=128)
    P.op("dve", lambda e: e.tensor_copy(m_rep3, m_scol.unsqueeze(2).to_broadcast([128, KC, 128])),
         reads=(T_mc,), writes=(T_mc,))
    P.dma("sp", lambda e: e.dma_start(out=m_nm.rearrange("p (l k) -> p l k", l=2),
                                      in_=norm_mix.rearrange("l (k p) -> p l k", p=128),
                                      allow_slow_non_contiguous=True), T_mn, writes=(T_mn,))
    P.dma("sp", lambda e: e.dma_start(out=m_nf.rearrange("p (l k) -> p l k", l=2),
                                      in_=norm_ffn.rearrange("l (k p) -> p l k", p=128),
                                      allow_slow_non_contiguous=True), T_mn, writes=(T_mn,), merge=True)
    P.dma("sp", lambda e: e.dma_start(out=v_gn, in_=g_norm.rearrange("l p -> p l"),
                                      allow_slow_non_contiguous=True), T_vec, writes=(T_vec,), merge=True)
    P.dma("sp", lambda e: e.dma_start(out=m_lb.rearrange("p (l r h) -> p l r h", l=2, r=2),
                                      in_=lb_raw.rearrange("l r (h p) -> p l r h", p=128),
                                      allow_slow_non_contiguous=True), T_mlb, writes=(T_mlb,))
    P.op("dve", lambda e: e.tensor_sub(m_lb[:, 0:16], m_lb[:, 16:32], m_lb[:, 0:16]), reads=(T_mlb,), writes=(T_mlb,))
    P.op("act", lambda e: e.activation(out=v_lb, in_=m_lb[:, 0:16], func=AF.Sigmoid), reads=(T_mlb,),
         writes=(T_vec,), merge=True)
    P.op("dve", lambda e: e.tensor_scalar(v_oml, v_lb, -1.0, 1.0, ALU.mult, ALU.add), reads=(T_vec,),
         writes=(T_vec,), merge=True)

    mi = 0
    for l in range(DEPTH):
        for j in range(6):
            for nb in range(4):
                i = mi % 2
                mi += 1
                n0 = j * D + nb * 512
                P.dma("sp", (lambda a, l_, n_: (lambda e: e.dma_start(
                    out=a.rearrange("p (k n) -> p k n", n=512),
                    in_=w_mod[l_, :, n_:n_ + 512].rearrange("(k p) n -> p k n", p=128))))(m_w[i], l, n0),
                    T_mw[i], writes=(T_mw[i],))
                P.dma("sp", (lambda a, l_, n_: (lambda e: e.dma_start(
                    out=a, in_=b_mod[l_, n_:n_ + 512].partition_broadcast(128))))(m_b[i], l, n0),
                    T_mb[i], writes=(T_mb[i],))
                bk = banks[i][:]
                w3 = m_w[i].rearrange("p (k n) -> p k n", n=512)
                for k in range(KC):
                    P.op("pe", (lambda b_, k_, w_: (lambda e: e.matmul(b_, m_rep3[:, k_, :], w_[:, k_, :],
                                                                      start=(k_ == 0), stop=(k_ == KC - 1))))(bk, k, w3),
                         reads=(T_mc, T_mw[i]), writes=(bank_t[i],), merge=(k > 0))
                if j in (2, 5):
                    P.op("dve", (lambda b_, mb_: (lambda e: e.tensor_add(m_row, b_, mb_)))(bk, m_b[i]),
                         reads=(bank_t[i], T_mb[i]), writes=(T_mrow,))
                    P.dma("sp", (lambda l_, j_, nb_: (lambda e: e.dma_start(
                        out=g_s[l_, j_ // 3, nb_ * 512:(nb_ + 1) * 512], in_=m_row[0:1, :])))(l, j, nb),
                        T_mrow, reads=(T_mrow,), writes=(T_gs,), merge=True)
                else:
                    P.op("dve", (lambda b_, mb_: (lambda e: e.tensor_add(m_row, b_, mb_)))(bk, m_b[i]),
                         reads=(bank_t[i], T_mb[i]), writes=(T_mrow,))
                    if j in (1, 4):
                        P.op("dve", lambda e: e.tensor_scalar_add(m_row, m_row, 1.0), reads=(T_mrow,), writes=(T_mrow,))
                    vdst = {0: v_sh1, 1: v_sc1, 3: v_sh2, 4: v_sc2}[j][l]
                    for c in range(4):
                        kc = nb * 4 + c
                        P.op("dve", (lambda c_: (lambda e: e.tensor_mul(m_tmp, m_row[:, c_ * 128:(c_ + 1) * 128], c_ident)))(c),
                             reads=(T_mrow, T_const), writes=(T_mtmp,))
                        P.op("dve", (lambda v_, kc_: (lambda e: e.reduce_sum(v_[:, kc_:kc_ + 1], m_tmp, AX.X)))(vdst, kc),
                             reads=(T_mtmp,), writes=(T_vec,), merge=True)
        P.op("dve", (lambda l_: (lambda e: e.tensor_mul(v_sc1[l_], v_sc1[l_], m_nm[:, l_ * KC:(l_ + 1) * KC])))(l),
             reads=(T_vec, T_mn), writes=(T_vec,), merge=True)
        P.op("dve", (lambda l_: (lambda e: e.tensor_mul(v_sc2[l_], v_sc2[l_], m_nf[:, l_ * KC:(l_ + 1) * KC])))(l),
             reads=(T_vec, T_mn), writes=(T_vec,), merge=True)
    P.barrier()

    def mk(f, *a):
        return lambda e: f(e, *a)

    bankbf = [b[:].bitcast(BF16) for b in banks]
    T_x = [P.tile("x%d" % i) for i in range(NTT)]
    T_scr = {}

    def dtl(*key):
        if key not in T_scr:
            T_scr[key] = P.tile(str(key))
        return T_scr[key]

    class Rot:
        def __init__(self, ids):
            self.ids = list(ids)
            self.i = 0

        def next(self):
            b = self.ids[self.i % len(self.ids)]
            self.i += 1
            return b

    def norm_T(xt_ap, T_xt, xn_ap, T_xn, junk_ap, T_junk, ss_ap, T_ss, hT3, T_hT, col0, scale_col, shift_col, first):
        P.op("act", lambda e: e.activation(out=junk_ap, in_=xt_ap, func=AF.Square, accum_out=ss_ap[:, 0:1]),
             reads=(T_xt,), writes=(T_junk, T_ss))
        P.op("act", lambda e: e.activation(out=ss_ap[:, 1:2], in_=ss_ap[:, 0:1], func=AF.Sqrt, scale=1.0 / D, bias=c_eps),
             reads=(T_ss, T_const), writes=(T_ss,))
        P.op("dve", lambda e: e.reciprocal(ss_ap[:, 2:3], ss_ap[:, 1:2]), reads=(T_ss,), writes=(T_ss,))
        P.op("dve", lambda e: e.tensor_scalar_mul(xn_ap, xt_ap, ss_ap[:, 2:3]), reads=(T_xt, T_ss), writes=(T_xn,))
        for half in range(2):
            b = half
            for j in range(8):
                kc = half * 8 + j
                P.op("pe", mk(lambda e, kc, j, b: e.transpose(bankbf[b][:, j * 128:(j + 1) * 128],
                                                              xn_ap[:, kc * 128:(kc + 1) * 128], c_identb), kc, j, b),
                     reads=(T_xn, T_const), writes=(bank_t[b],), merge=(j > 0))
            for j in range(8):
                kc = half * 8 + j
                dst = hT3[:, kc, col0:col0 + 128]
                src = bankbf[b][:, j * 128:(j + 1) * 128]
                mg = not (first and half == 0 and j == 0)
                if half == 0:
                    P.op("act", mk(lambda e, d_, s_, kc: e.activation(out=d_, in_=s_, func=AF.Identity,
                                                                      scale=scale_col[:, kc:kc + 1],
                                                                      bias=shift_col[:, kc:kc + 1]), dst, src, kc),
                         reads=(bank_t[b], T_vec), writes=(T_hT,), merge=mg)
                else:
                    P.op("dve", mk(lambda e, d_, s_, kc: e.tensor_scalar(d_, s_, scale_col[:, kc:kc + 1],
                                                                         shift_col[:, kc:kc + 1], ALU.mult, ALU.add),
                                   dst, src, kc),
                         reads=(bank_t[b], T_vec), writes=(T_hT,), merge=mg)

    import os
    KP1 = int(os.environ.get('KP1', '9'))

    def phase_P1(l):
        SB.reset(PERSIST)
        hT = SB.bf16(KC * TB)
        hT3 = hT.rearrange("p (k t) -> p k t", k=KC)
        xt = [SB.f32(D) for _ in range(2)]
        pet = [SB.f32(D) for _ in range(2)]
        xn = [SB.bf16(D) for _ in range(2)]
        junk = SB.bf16(D)
        ss = [SB.f32(4) for _ in range(2)]
        wt = [SB.bf16(KC * 512) for _ in range(2)]
        so = [SB.f32(TB) for _ in range(3)]
        T_hT = P.tile("hT")
        T_xt = [P.tile("xt%d" % i) for i in range(2)]
        T_pet = [P.tile("pet%d" % i) for i in range(2)]
        T_xn = [P.tile("xn%d" % i) for i in range(2)]
        T_junk = P.tile("junk")
        T_ss = [P.tile("ss%d" % i) for i in range(2)]
        T_wt = [P.tile("wt%d" % i) for i in range(2)]
        T_so = [P.tile("so%d" % i) for i in range(3)]
        rot = Rot([2, 3, 4, 5, 6, 7])
        soi = [0]
        NCG = N_IN // 512
        cgen = None
        if BGCONV and l == 0 and DEPTH > 1:
            stg_f1 = [SB.f32(2048) for _ in range(NST)]
            stg_b1 = [SB.bf16(2048) for _ in range(NST)]
            cgen = conv_gen(1, stg_f1, stg_b1, [P.tile("sf1_%d" % i) for i in range(NST)],
                            [P.tile("sb1_%d" % i) for i in range(NST)], "sp", "pool", ["dve", "act"])

        def wload(cg):
            i = cg % 2
            P.dma("sp", mk(lambda e, i, cg: e.dma_start(
                out=wt[i].rearrange("p (k n) -> p k n", n=512),
                in_=wb_in[l, :, cg * 512:(cg + 1) * 512].rearrange("(k p) n -> p k n", p=128)), i, cg),
                T_wt[i], reads=(T_w[("in", l)],), writes=(T_wt[i],))

        for tb in range(NB):
            tok0 = tb * TB
            wload(0)
            for tt in range(TBT):
                i = tt % 2
                g = tb * TBT + tt
                r0 = g * 128
                src = xin if l == 0 else xs
                P.dma("sp", mk(lambda e, i, r0, src: e.dma_start(out=xt[i], in_=src[r0:r0 + 128, :]), i, r0, src),
                      T_xt[i], reads=(() if l == 0 else (T_x[g],)), writes=(T_xt[i],))
                if l == 0:
                    P.dma("sp", mk(lambda e, i, r0: e.dma_start(out=pet[i], in_=pe_in[r0:r0 + 128, :]), i, r0),
                          T_pet[i], writes=(T_pet[i],))
                    P.op("pool", mk(lambda e, i: e.tensor_add(xt[i], xt[i], pet[i]), i),
                         reads=(T_xt[i], T_pet[i]), writes=(T_xt[i],))
                    P.dma("pool", mk(lambda e, i, r0: e.dma_start(out=xs[r0:r0 + 128, :], in_=xt[i]), i, r0),
                          T_xt[i], reads=(T_xt[i],), writes=(T_x[g],))
                if KP1 >= 1:
                    norm_T(xt[i], T_xt[i], xn[i], T_xn[i], junk, T_junk, ss[i], T_ss[i], hT3, T_hT, tt * 128,
                           v_sc1[l], v_sh1[l], first=(tt == 0))
            for cg in range(NCG if KP1 >= 2 else 0):
                if cg + 1 < NCG:
                    wload(cg + 1)
                w3 = wt[cg % 2].rearrange("p (k n) -> p k n", n=512)
                Tw = T_wt[cg % 2]
                if cg in (6, 7):
                    for tt in range(TBT):
                        b = rot.next()
                        for k in range(KC):
                            P.op("pe", mk(lambda e, b, k, tt, w3: e.matmul(banks[b][:], hT3[:, k, tt * 128:(tt + 1) * 128],
                                                                          w3[:, k, :], start=(k == 0), stop=(k == KC - 1)),
                                          b, k, tt, w3),
                                 reads=(T_hT, Tw), writes=(bank_t[b],), merge=(k > 0))
                        j = soi[0] % 3
                        soi[0] += 1
                        sb_ = so[j].bitcast(BF16)[:, 0:512]
                        P.op("dve" if tt % 2 else "act",
                             mk((lambda e, sb_, b: e.tensor_copy(sb_, banks[b][:])) if tt % 2 else
                                (lambda e, sb_, b: e.copy(sb_, banks[b][:])), sb_, b),
                             reads=(bank_t[b],), writes=(T_so[j],))
                        r0 = tok0 + tt * 128
                        c0 = (cg - 6) * 512
                        P.dma("pool", mk(lambda e, sb_, r0, c0: e.dma_start(out=vv[r0:r0 + 128, c0:c0 + 512], in_=sb_),
                                         sb_, r0, c0),
                              T_so[j], reads=(T_so[j],), writes=(dtl("vv", r0, c0),))
                    continue
                for cc in range(4):
                    j = soi[0] % 3
                    soi[0] += 1
                    isf32 = cg < 6
                    stg = so[j] if isf32 else so[j].bitcast(BF16)[:, 0:TB]
                    for hf in range(NH5):
                        b = rot.next()
                        for k in range(KC):
                            P.op("pe", mk(lambda e, b, k, cc, hf, w3: e.matmul(
                                banks[b][:], w3[:, k, cc * 128:(cc + 1) * 128], hT3[:, k, hf * 512:(hf + 1) * 512],
                                start=(k == 0), stop=(k == KC - 1)), b, k, cc, hf, w3),
                                reads=(T_hT, Tw), writes=(bank_t[b],), merge=(k > 0))
                        dst = stg[:, hf * 512:(hf + 1) * 512]
                        if cg in (8, 9):
                            P.op("act", mk(lambda e, d_, b: e.activation(out=d_, in_=banks[b][:], func=AF.Silu), dst, b),
                                 reads=(bank_t[b],), writes=(T_so[j],), merge=(hf > 0))
                        elif cg >= 12:
                            P.op("act", mk(lambda e, d_, b: e.activation(out=d_, in_=banks[b][:], func=AF.Sigmoid), dst, b),
                                 reads=(bank_t[b],), writes=(T_so[j],), merge=(hf > 0))
                        elif (cc + hf) % 2 == 0:
                            P.op("dve", mk(lambda e, d_, b: e.tensor_copy(d_, banks[b][:]), dst, b),
                                 reads=(bank_t[b],), writes=(T_so[j],), merge=(hf > 0))
                        else:
                            P.op("act", mk(lambda e, d_, b: e.copy(d_, banks[b][:]), dst, b),
                                 reads=(bank_t[b],), writes=(T_so[j],), merge=(hf > 0))
                    if isf32:
                        fc = cg * 4 + cc
                        P.dma("pool", mk(lambda e, stg, fc, tok0: e.dma_start(out=pq[fc, :, tok0:tok0 + TB], in_=stg),
                                         stg, fc, tok0),
                              T_so[j], reads=(T_so[j],), writes=(dtl("pq", fc, tb),))
                    else:
                        fc = (cg - 8) * 4 + cc
                        P.dma("pool", mk(lambda e, stg, fc, tok0: e.dma_start(out=pb[fc, :, tok0:tok0 + TB], in_=stg),
                                         stg, fc, tok0),
                              T_so[j], reads=(T_so[j],), writes=(dtl("pb", fc, tb),))
                if cgen is not None:
                    for _ in range(4):
                        next(cgen, None)
        if cgen is not None:
            for _ in cgen:
                pass
        P.barrier()

    KP2 = int(os.environ.get('KP2', '9'))

    def phase_P2a(l):
        SB.reset(PERSIST)
        NH = min(512, NT)
        NHH = NT // NH
        qf = SB.f32(NT)
        gin = [SB.f32(NT) for _ in range(2)]
        vtok = SB.bf16(NT)
        vtok3 = vtok.rearrange("p (t c) -> p t c", c=128)
        qt = [SB.bf16(NT) for _ in range(2)]
        kt = [SB.bf16(NT) for _ in range(2)]
        khT = [SB.bf16(NT) for _ in range(2)]
        sog = khT[0]
        vm = [SB.bf16(NT) for _ in range(4)]
        vm3 = [a_.rearrange("p (t c) -> p t c", c=128) for a_ in vm]
        kh = [SB.bf16(NT) for _ in range(2)]
        kh3 = [a.rearrange("p (t c) -> p t c", c=128) for a in kh]
        el = [SB.f32(NCHK) for _ in range(2)]
        em = [SB.f32(NCHK) for _ in range(2)]
        wksets = [[SB.f32(NH) for _ in range(4)] for _ in range(2)]
        cmk = SB.f32(NH)
        S = [SB.f32(128) for _ in range(2)]
        NR = 8
        Sb = [[SB.bf16(128) for _ in range(NR)] for _ in range(2)]
        AT = [[SB.bf16(128) for _ in range(2)] for _ in range(2)]
        sout = [SB.f32(128) for _ in range(4)]
        ostg = [SB.bf16(512) for _ in range(2)]
        T_q = P.tile("q")
        T_g = [P.tile("g%d" % d) for d in range(2)]
        T_v = P.tile("v")
        T_vm = P.tile("vm")
        T_qt = [P.tile("qt%d" % d) for d in range(2)]
        T_kt = [P.tile("kt%d" % d) for d in range(2)]
        T_khT = [P.tile("khT%d" % d) for d in range(2)]
        T_sog = T_khT[0]
        T_kh = [P.tile("kh%d" % d) for d in range(2)]
        T_el = [P.tile("el%d" % d) for d in range(2)]
        T_wksets = [[P.tile("wk%d_%d" % (s_, i)) for i in range(4)] for s_ in range(2)]
        T_cmk = P.tile("cmk")
        T_S = [P.tile("S%d" % d) for d in range(2)]
        T_Sb = [[P.tile("Sb%d_%d" % (d, i)) for i in range(NR)] for d in range(2)]
        T_AT = [[P.tile("AT%d_%d" % (d, i)) for i in range(2)] for d in range(2)]
        T_sout = [P.tile("sout%d" % i) for i in range(4)]
        T_ostg = [P.tile("ostg%d" % i) for i in range(2)]
        souti = [0]
        ostgi = [0]
        pit = [0]

        P.op("dve", lambda e: e.memset(cmk, 1.0), writes=(T_cmk,))
        for d in range(2):
            for par in range(2):
                P.op("dve", mk(lambda e, d, par: e.memset(AT[d][par], 0.0), d, par), writes=(T_AT[d][par],))

        for h in range(H_A):
            P.dma("sp", mk(lambda e, h: e.dma_start(out=qf, in_=pq[h]), h), T_q, writes=(T_q,))
            for d in range(2):
                P.dma("sp", mk(lambda e, h, d: e.dma_start(out=gin[d], in_=pq[8 + 8 * d + h]), h, d), T_g[d], writes=(T_g[d],))
            P.dma("sp", mk(lambda e, h: e.dma_start(out=vtok3, in_=vv[:, h * 128:(h + 1) * 128].rearrange("(t p) c -> p t c", p=128)), h),
                  T_v, writes=(T_v,))
            for c in range(4):
                if c % 2 == 0:
                    P.op("dve", mk(lambda e, c: e.tensor_scalar_mul(vm[c], vtok, c_rm[:, c:c + 1]), c), reads=(T_v, T_const),
                         writes=(T_vm,), merge=(c > 0))
                else:
                    P.op("act", mk(lambda e, c: e.activation(out=vm[c], in_=vtok, func=AF.Copy, scale=c_rm[:, c:c + 1]), c),
                         reads=(T_v, T_const), writes=(T_vm,), merge=(c > 0))
            for d in range(2):
                P.dma("sp", mk(lambda e, h, d: e.dma_start(out=S[d], in_=s0[l, d, h]), h, d), T_S[d], writes=(T_S[d],))
            for d in range(2 if KP2 >= 1 else 0):
                if l == 0:
                    lbv, omlv, nomlv = 0.0, 1.0, -1.0
                else:
                    lbv = v_lb[:, d * 8 + h:d * 8 + h + 1]
                    omlv = v_oml[:, d * 8 + h:d * 8 + h + 1]
                def prep_iter(d, hh, wk, T_wk, lbv, omlv):
                    c0 = hh * NH
                    A, B, C, Dd = wk
                    TA, TB_, TC, TD = T_wk
                    gsl = gin[d][:, c0:c0 + NH]
                    P.op("act", mk(lambda e, gsl: e.activation(out=A, in_=gsl, func=AF.Sigmoid), gsl),
                         reads=(T_g[d],), writes=(TA,))
                    if l == 0:
                        P.op("dve", lambda e: e.tensor_scalar(C, A, -1.0, 1.0, ALU.mult, ALU.add), reads=(TA,), writes=(TC,))
                        P.op("act", lambda e: e.activation(out=Dd, in_=A, func=AF.Ln), reads=(TA,), writes=(TD,))
                    else:
                        P.op("dve", mk(lambda e, omlv, lbv: e.tensor_scalar(B, A, omlv, lbv, ALU.mult, ALU.add), omlv, lbv),
                             reads=(TA, T_vec), writes=(TB_,))
                        P.op("dve", mk(lambda e, omlv: e.tensor_scalar(C, A, -1.0, omlv, ALU.add, ALU.mult), omlv),
                             reads=(TA, T_vec), writes=(TC,))
                        P.op("dve", lambda e: e.tensor_scalar_mul(C, C, -1.0), reads=(TC,), writes=(TC,))
                        P.op("act", lambda e: e.activation(out=Dd, in_=B, func=AF.Ln), reads=(TB_,), writes=(TD,))
                    P.op("dve", lambda e: e.tensor_tensor_scan(B, cmk, Dd, 0.0, ALU.mult, ALU.add),
                         reads=(T_cmk, TD), writes=(TB_,))
                    B3 = B.rearrange("p (c t) -> p c t", t=CH)
                    A3 = A.rearrange("p (c t) -> p c t", t=CH)
                    D3 = Dd.rearrange("p (c t) -> p c t", t=CH)
                    nch = NH // CH
                    if d == 0:
                        P.op("dve", lambda e: e.tensor_tensor(A3, B3[:, :, 0:1].to_broadcast([128, nch, CH]), B3, ALU.subtract),
                             reads=(TB_,), writes=(TA,))
                        P.op("dve", lambda e: e.tensor_tensor(A3, D3[:, :, 0:1].to_broadcast([128, nch, CH]), A3, ALU.subtract),
                             reads=(TA, TD), writes=(TA,))
                        X, TX, X3, Y, TY, Y3 = A, TA, A3, B, TB_, B3
                        lastc = CH - 1
                    else:
                        P.op("dve", lambda e: e.tensor_tensor(A3, B3[:, :, CH - 1:CH].to_broadcast([128, nch, CH]), B3, ALU.subtract),
                             reads=(TB_,), writes=(TA,))
                        P.op("dve", lambda e: e.tensor_add(A, A, Dd), reads=(TA, TD), writes=(TA,))
                        X, TX, X3, Y, TY, Y3 = A, TA, A3, B, TB_, B3
                        lastc = 0
                    MID = CH // 2
                    P.op("act", mk(lambda e, X3, d, hh, lc: e.activation(out=el[d][:, hh * nch:(hh + 1) * nch].unsqueeze(2),
                                                                        in_=X3[:, :, lc:lc + 1], func=AF.Exp), X3, d, hh, lastc),
                         reads=(TX,), writes=(T_el[d],), merge=(hh > 0))
                    P.op("act", mk(lambda e, X3, d, hh: e.activation(out=em[d][:, hh * nch:(hh + 1) * nch].unsqueeze(2),
                                                                    in_=X3[:, :, MID:MID + 1], func=AF.Exp), X3, d, hh),
                         reads=(TX,), writes=(T_el[d],), merge=True)
                    P.op("dve", mk(lambda e, X3, lc: e.tensor_tensor(D3, X3[:, :, lc:lc + 1].to_broadcast([128, nch, CH]), X3,
                                                                     ALU.subtract), X3, lastc),
                         reads=(TX,), writes=(TD,))
                    P.op("act", lambda e: e.activation(out=Dd, in_=Dd, func=AF.Exp), reads=(TD,), writes=(TD,))
                    P.op("pool", mk(lambda e, d, c0: e.tensor_mul(khT[d][:, c0:c0 + NH], C, Dd), d, c0),
                         reads=(TC, TD), writes=(T_khT[d],), merge=(hh > 0))
                    P.op("dve", mk(lambda e, X3: e.tensor_tensor(D3, X3[:, :, MID:MID + 1].to_broadcast([128, nch, CH]), X3,
                                                                 ALU.subtract), X3),
                         reads=(TX,), writes=(TD,))
                    P.op("act", mk(lambda e, Y: e.activation(out=Y, in_=Dd, func=AF.Exp), Y), reads=(TD,), writes=(TY,))
                    P.op("pool", mk(lambda e, Y, d, c0: e.tensor_mul(kt[d][:, c0:c0 + NH], C, Y), Y, d, c0),
                         reads=(TC, TY), writes=(T_kt[d],), merge=(hh > 0))
                    P.op("act", mk(lambda e, Y: e.activation(out=Y, in_=Dd, func=AF.Exp, scale=-1.0), Y), reads=(TD,), writes=(TY,))
                    P.op("pool", mk(lambda e, Y, d, c0: e.tensor_mul(qt[d][:, c0:c0 + NH], qf[:, c0:c0 + NH], Y), Y, d, c0),
                         reads=(T_q, TY), writes=(T_qt[d],), merge=(hh > 0))
                for hh in range(0, NHH, 2):
                    caps = []
                    for q_ in range(min(2, NHH - hh)):
                        P.capture()
                        prep_iter(d, hh + q_, wksets[q_], T_wksets[q_], lbv, omlv)
                        caps.append(P.end_capture())
                    P.replay(caps)
                for g8 in range(0, NTT if KP2 >= 2 else 0, 8):
                    b = (g8 // 8) % 2
                    n8 = min(8, NTT - g8)
                    for j in range(n8):
                        P.op("pe", mk(lambda e, b, j, g8, d: e.transpose(bankbf[b][:, j * 128:(j + 1) * 128],
                                                                          khT[d][:, (g8 + j) * 128:(g8 + j + 1) * 128], c_identb),
                                      b, j, g8, d),
                             reads=(T_khT[d], T_const), writes=(bank_t[b],), merge=(j > 0))
                    P.op("act" if b else "dve",
                         mk((lambda e, b, g8, n8, d: e.copy(kh[d][:, g8 * 128:(g8 + n8) * 128], bankbf[b][:, 0:n8 * 128])) if b else
                            (lambda e, b, g8, n8, d: e.tensor_copy(kh[d][:, g8 * 128:(g8 + n8) * 128], bankbf[b][:, 0:n8 * 128])),
                            b, g8, n8, d),
                         reads=(bank_t[b],), writes=(T_kh[d],), merge=(g8 > 0))
            if debug and l == 0 and h == 0:
                for d in range(2):
                    for nm, src_, tl_ in (("qt", qt[d], T_qt[d]), ("kt", kt[d], T_kt[d]), ("khT", khT[d], T_khT[d]), ("kh", kh[d], T_kh[d])):
                        dd = nc.dram_tensor("dbg_%s%d" % (nm, d), [128, NT], BF16, kind="ExternalOutput").ap()
                        P.dma("sp", mk(lambda e, dd, src_: e.dma_start(out=dd, in_=src_), dd, src_), tl_, reads=(tl_,))
                    dd = nc.dram_tensor("dbg_el%d" % d, [128, NCHK], F32, kind="ExternalOutput").ap()
                    P.dma("sp", mk(lambda e, dd, d: e.dma_start(out=dd, in_=el[d]), dd, d), T_el[d], reads=(T_el[d],))
            P.dma("sp", mk(lambda e, h: e.dma_start(out=sog, in_=pb[h]), h), T_sog, writes=(T_sog,))
            ring = [0, 0]
            for d in range(2):
                g0 = 0 if d == 0 else NCHK - 1
                P.op("act", mk(lambda e, d, g0: e.activation(out=Sb[d][0], in_=S[d], func=AF.Copy, scale=em[d][:, g0:g0 + 1]), d, g0),
                     reads=(T_S[d], T_el[d]), writes=(T_Sb[d][0],))
            for i in range(NTT if KP2 >= 3 else 0):
                tiles = (i, NTT - 1 - i)
                par = i % 2
                for d in range(2):
                    j = tiles[d]
                    bA = d
                    P.op("pe", mk(lambda e, bA, d, j: e.matmul(banks[bA][:, 0:128], kt[d][:, j * 128:(j + 1) * 128],
                                                               qt[d][:, j * 128:(j + 1) * 128], start=True, stop=True), bA, d, j),
                         reads=(T_kt[d], T_qt[d]), writes=(bank_t[bA],))
                    P.op("dve", mk(lambda e, bA, d, par: e.copy_predicated(AT[d][par], (c_mf if d == 0 else c_mb).bitcast(mybir.dt.uint32),
                                                                            banks[bA][:, 0:128]), bA, d, par),
                         reads=(bank_t[bA], T_const), writes=(T_AT[d][par],))
                    bU = 2 + d * 2 + par
                    for c in range(4):
                        P.op("pe", mk(lambda e, bU, c, d, j: e.matmul(banks[bU][:, c * 128:(c + 1) * 128],
                                                                      kh3[d][:, j, :], vm3[c][:, j, :],
                                                                      start=True, stop=True), bU, c, d, j),
                             reads=(T_kh[d], T_vm), writes=(bank_t[bU],), merge=(c > 0))
                par = i % 2
                for d in range(2):
                    j = tiles[d]
                    bO = 6 + d
                    P.op("pe", mk(lambda e, bO, d, j, par: e.matmul(banks[bO][:, 0:128], vtok3[:, j, :], AT[d][par],
                                                                    start=True, stop=False), bO, d, j, par),
                         reads=(T_v, T_AT[d][par]), writes=(bank_t[bO],))
                for n_ in range(4):
                    for d in range(2):
                        j = tiles[d]
                        bU = 2 + d * 2 + par
                        bO = 6 + d
                        c = n_ if d == 0 else 3 - n_
                        gc = j * 4 + c
                        r = ring[d]
                        col = j * 128 + c * 32
                        P.op("pe", mk(lambda e, bO, c, d, r, col, n_: e.matmul(banks[bO][:, c * 32:(c + 1) * 32], Sb[d][r],
                                                                               qt[d][:, col:col + 32], start=False, stop=(n_ == 3)),
                                      bO, c, d, r, col, n_),
                             reads=(T_Sb[d][r], T_qt[d]), writes=(bank_t[bO],), merge=True)
                        P.op("dve", mk(lambda e, d, gc, bU, c: e.scalar_tensor_tensor(S[d], S[d], el[d][:, gc:gc + 1],
                                                                                      banks[bU][:, c * 128:(c + 1) * 128],
                                                                                      ALU.mult, ALU.add), d, gc, bU, c),
                             reads=(T_S[d], T_el[d], bank_t[bU]), writes=(T_S[d],))
                        seg_end = (gc % 8 == 7) if d == 0 else (gc % 8 == 0)
                        if seg_end:
                            seg = gc // 8
                            so_i = souti[0] % 4
                            souti[0] += 1
                            P.op("act", mk(lambda e, so_i, d: e.copy(sout[so_i], S[d]), so_i, d), reads=(T_S[d],),
                                 writes=(T_sout[so_i],))
                            P.dma("pool", mk(lambda e, so_i, seg, d, h: e.dma_start(out=st_out[l, seg, d, h], in_=sout[so_i]),
                                             so_i, seg, d, h),
                                  T_sout[so_i], reads=(T_sout[so_i],))
                            last_seg = (seg == NSEG - 1) if d == 0 else (seg == 0)
                            if not last_seg:
                                P.op("dve", mk(lambda e, d: e.tensor_scalar_mul(S[d], S[d], c_carry[:, 0:1]), d),
                                     reads=(T_S[d], T_const), writes=(T_S[d],))
                        ring[d] = (r + 1) % NR
                        r2 = ring[d]
                        gn = gc + 1 if d == 0 else gc - 1
                        if 0 <= gn < NCHK:
                            P.op("act", mk(lambda e, d, r2, gn: e.activation(out=Sb[d][r2], in_=S[d], func=AF.Copy,
                                                                             scale=em[d][:, gn:gn + 1]), d, r2, gn),
                                 reads=(T_S[d], T_el[d]), writes=(T_Sb[d][r2],))
                for d in range(2):
                    j = tiles[d]
                    bO = 6 + d
                    P.op("act", mk(lambda e, d, j, bO: e.copy(gin[d][:, j * 128:(j + 1) * 128], banks[bO][:, 0:128]), d, j, bO),
                         reads=(bank_t[bO],), writes=(T_g[d],), merge=True)
            if debug and l == 0 and h == 0:
                for d in range(2):
                    dd = nc.dram_tensor("dbg_o%d" % d, [128, NT], F32, kind="ExternalOutput").ap()
                    P.dma("sp", mk(lambda e, dd, d: e.dma_start(out=dd, in_=gin[d]), dd, d), T_g[d], reads=(T_g[d],))
            def hn_block(blk, bufs, tls, h):
                c0 = blk * 512
                o_, sq, rs = bufs[0], bufs[1], bufs[2]
                To, Tq_, Tr = tls[0], tls[1], tls[2]
                P.op("dve", lambda e: e.tensor_add(o_, gin[0][:, c0:c0 + 512], gin[1][:, c0:c0 + 512]),
                     reads=(T_g[0], T_g[1]), writes=(To,))
                P.op("act", lambda e: e.activation(out=sq, in_=o_, func=AF.Square), reads=(To,), writes=(Tq_,))
                bN = blk % 2
                P.op("pe", lambda e: e.matmul(banks[bN][:], c_ones, sq, start=True, stop=True),
                     reads=(T_const, Tq_), writes=(bank_t[bN],))
                P.op("act", lambda e: e.activation(out=rs, in_=banks[bN][:], func=AF.Sqrt, scale=1.0 / 128, bias=c_eps),
                     reads=(bank_t[bN], T_const), writes=(Tr,))
                P.op("dve", lambda e: e.reciprocal(rs, rs), reads=(Tr,), writes=(Tr,))
                P.op("dve", lambda e: e.tensor_mul(o_, o_, rs), reads=(To, Tr), writes=(To,))
                oi = ostgi[0] % 2
                ostgi[0] += 1
                P.op("dve", lambda e: e.scalar_tensor_tensor(ostg[oi], o_, v_gn[:, l:l + 1], sog[:, c0:c0 + 512], ALU.mult, ALU.mult),
                     reads=(To, T_vec, T_sog), writes=(T_ostg[oi],))
                P.dma("pool", lambda e: e.dma_start(out=og_s[h, :, c0:c0 + 512], in_=ostg[oi]), T_ostg[oi], reads=(T_ostg[oi],))

            for blk in range(0, NT // 512 if KP2 >= 4 else 0, 2):
                caps = []
                for q_ in range(min(2, NT // 512 - blk)):
                    P.capture()
                    hn_block(blk + q_, wksets[q_], T_wksets[q_], h)
                    caps.append(P.end_capture())
                P.replay(caps)
        P.barrier()

    def phase_P2b(l):
        SB.reset(PERSIST)
        NG = 2 if NT > 2048 else 4
        cf = SB.bf16(2 * 512)
        cf3 = cf.rearrange("p (k n) -> p k n", n=512)
        Aall = SB.bf16(NTT * NG * 512)
        A4 = Aall.rearrange("p (t g n) -> p t g n", g=NG, n=512)
        ut = [SB.bf16(2 * 512) for _ in range(2)]
        NDT = 4
        dt_ = [SB.bf16(NDT * 2 * 512) for _ in range(3)]
        zst = [SB.bf16(512) for _ in range(4)]
        T_cf = P.tile("cf")
        T_A = P.tile("Aall")
        T_ut = [P.tile("ut%d" % i) for i in range(2)]
        T_dt = [P.tile("dt%d" % i) for i in range(3)]
        T_zst = [P.tile("zst%d" % i) for i in range(4)]
        P.dma("sp", lambda e: e.dma_start(out=cf3, in_=cfsf.rearrange("(k p) n -> p k n", p=128)), T_cf, writes=(T_cf,))
        uti = [0]
        dti = [0]
        zi = [0]
        rot = Rot(range(8))
        for gp in range(4 // NG):
            first = True
            for g in range(NG):
                gg = gp * NG + g
                for t5 in range(NT // 512):
                    i = uti[0] % 2
                    uti[0] += 1
                    u3 = ut[i].rearrange("p (k n) -> p k n", n=512)
                    P.dma("sp", mk(lambda e, u3, gg, t5: e.dma_start(out=u3, in_=pb[8 + 2 * gg:8 + 2 * gg + 2, :, t5 * 512:(t5 + 1) * 512]
                                                                    .rearrange("k p n -> p k n")), u3, gg, t5),
                          T_ut[i], writes=(T_ut[i],))
                    for t1 in range(4):
                        tt = t5 * 4 + t1
                        b = rot.next()
                        for k in range(2):
                            P.op("pe", mk(lambda e, b, u3, k, t1: e.matmul(banks[b][:], u3[:, k, t1 * 128:(t1 + 1) * 128], cf3[:, k, :],
                                                                          start=(k == 0), stop=(k == 1)), b, u3, k, t1),
                                 reads=(T_ut[i], T_cf), writes=(bank_t[b],), merge=(k > 0))
                        P.op("act" if tt % 2 else "dve",
                             mk((lambda e, b, tt, g: e.copy(A4[:, tt, g, :], banks[b][:])) if tt % 2 else
                                (lambda e, b, tt, g: e.tensor_copy(A4[:, tt, g, :], banks[b][:])), b, tt, g),
                             reads=(bank_t[b],), writes=(T_A,), merge=(not first))
                        first = False
            for kb in range(NT // 512):
                nacc = 2 * NG
                bs = [(kb % 2) * 4 + a for a in range(nacc)] if nacc <= 4 else list(range(8))
                for n0 in range(0, NTT, NDT):
                    nn = min(NDT, NTT - n0)
                    i = dti[0] % 3
                    dti[0] += 1
                    d4 = dt_[i].rearrange("p (t c n) -> p t c n", c=2, n=512)
                    for cs in range(2):
                        P.dma("sp", mk(lambda e, d4, cs, n0, nn, kb: e.dma_start(
                            out=d4[:, 0:nn, cs, :],
                            in_=dn[cs, n0 * 128:(n0 + nn) * 128, kb * 512:(kb + 1) * 512].rearrange("(t p) n -> p t n", p=128)),
                            d4, cs, n0, nn, kb), T_dt[i], writes=(T_dt[i],), merge=(cs > 0))
                    for t in range(nn):
                        nt_ = n0 + t
                        for cs in range(2):
                            for g in range(NG):
                                for mc in range(2):
                                    b = bs[g * 2 + mc]
                                    fst = (nt_ == 0 and cs == 0)
                                    lst = (nt_ == NTT - 1 and cs == 1)
                                    P.op("pe", mk(lambda e, b, nt_, g, cs, mc, d4, t, fst, lst: e.matmul(
                                        banks[b][:], A4[:, nt_, g, cs * 256 + mc * 128:cs * 256 + (mc + 1) * 128], d4[:, t, cs, :],
                                        start=fst, stop=lst), b, nt_, g, cs, mc, d4, t, fst, lst),
                                        reads=(T_A, T_dt[i]), writes=(bank_t[b],), merge=(not fst))
                for g in range(NG):
                    for mc in range(2):
                        b = bs[g * 2 + mc]
                        j = zi[0] % 4
                        zi[0] += 1
                        P.op("act" if mc else "dve",
                             mk((lambda e, j, b: e.copy(zst[j], banks[b][:])) if mc else
                                (lambda e, j, b: e.tensor_copy(zst[j], banks[b][:])), j, b),
                             reads=(bank_t[b],), writes=(T_zst[j],))
                        zc = (gp * NG + g) * 2 + mc
                        P.dma("pool", mk(lambda e, j, zc, kb: e.dma_start(out=z_s[zc, :, kb * 512:(kb + 1) * 512], in_=zst[j]), j, zc, kb),
                              T_zst[j], reads=(T_zst[j],))
        P.barrier()

    def phase_P3(l):
        SB.reset(PERSIST)
        TB3 = 512
        NB3 = NT // TB3
        ogT = SB.bf16(8 * TB3)
        zT = SB.bf16(8 * TB3)
        og3 = ogT.rearrange("p (k t) -> p k t", k=8)
        z3 = zT.rearrange("p (k t) -> p k t", k=8)
        mh = SB.bf16(KC * TB3)
        mh3 = mh.rearrange("p (k t) -> p k t", k=KC)
        actT = SB.bf16(FC * TB3)
        act3 = actT.rearrange("p (k t) -> p k t", k=FC)
        wt_all = SB.bf16(2 * KC * 512)
        wh = [wt_all[:, i * 4096:(i + 1) * 4096] for i in range(4)]
        sg = [SB.bf16(4 * TB3) for _ in range(2)]
        gb1 = SB.f32(D)
        gb2 = SB.f32(D)
        nfb = SB.f32(D)
        xt = [SB.f32(D) for _ in range(4)]
        xn = [SB.bf16(D)] * 2
        junk = SB.bf16(D)
        ss = [SB.f32(4) for _ in range(2)]
        tmp = [SB.f32(512) for _ in range(3)]
        T_og = P.tile("ogT")
        T_z = P.tile("zT")
        T_mh = P.tile("mh")
        T_act = P.tile("actT")
        T_wh = [P.tile("wh%d" % i) for i in range(4)]
        T_sg = [P.tile("sg%d" % i) for i in range(2)]
        T_gb = P.tile("gb")
        T_xt = [P.tile("xt%d" % i) for i in range(4)]
        T_xn = [P.tile("xn")] * 2
        T_junk = P.tile("junk")
        T_ss = [P.tile("ss%d" % i) for i in range(2)]
        T_tmp = [P.tile("tmp%d" % i) for i in range(3)]
        rot = Rot([2, 3, 4, 5, 6, 7])
        wi = [0]
        sgi = [0]
        tmi = [0]
        last = (l == DEPTH - 1)
        P.dma("sp", lambda e: e.dma_start(out=gb1, in_=g_s[l, 0].partition_broadcast(128)), T_gb, writes=(T_gb,))
        P.dma("sp", lambda e: e.dma_start(out=gb2, in_=g_s[l, 1].partition_broadcast(128)), T_gb, writes=(T_gb,), merge=True)
        if last:
            P.dma("sp", lambda e: e.dma_start(out=nfb, in_=norm_final.partition_broadcast(128)), T_gb, writes=(T_gb,), merge=True)

        def wload(src, r0, nk, c0, key, ncol=512):
            if nk * ncol <= 4096:
                i = wi[0] % 4
                wi[0] += 1
                buf, tls = wh[i], (T_wh[i],)
            else:
                if wi[0] % 2:
                    wi[0] += 1
                i = wi[0] % 4
                wi[0] += 2
                buf, tls = wt_all[:, i * 4096:(i + 2) * 4096], (T_wh[i], T_wh[i + 1])
            w3 = buf[:, 0:nk * ncol].rearrange("p (k n) -> p k n", n=ncol)
            P.dma("sp", lambda e: e.dma_start(out=w3, in_=src[r0:r0 + nk * 128, c0:c0 + ncol].rearrange("(k p) n -> p k n", p=128)),
                  tls[0], reads=(T_w[key],), writes=tls)
            return w3, tls

        for tb in range(NB3):
            tok0 = tb * TB3
            P.dma("sp", mk(lambda e, tok0: e.dma_start(out=og3, in_=og_s[:, :, tok0:tok0 + TB3].rearrange("k p n -> p k n")), tok0),
                  T_og, writes=(T_og,))
            P.dma("sp", mk(lambda e, tok0: e.dma_start(out=z3, in_=z_s[:, :, tok0:tok0 + TB3].rearrange("k p n -> p k n")), tok0),
                  T_z, writes=(T_z,))
            for tt in range(4):
                r0 = tok0 + tt * 128
                P.dma("sp", mk(lambda e, tt, r0: e.dma_start(out=xt[tt], in_=xs[r0:r0 + 128, :]), tt, r0),
                      T_xt[tt], reads=(T_x[r0 // 128],), writes=(T_xt[tt],))
            for dg in range(4):
                wa3, Twa = wload(wb_a[l], 0, 8, dg * 512, ("a", l))
                wb3, Twb = wload(wb_b[l], 0, 8, dg * 512, ("b", l))
                sga_i = sgi[0] % 2
                sgi[0] += 1
                sga = sg[sga_i].rearrange("p (k t) -> p k t", k=4)
                P.dma("sp", mk(lambda e, sga, dg, tok0: e.dma_start(
                    out=sga, in_=pb[16 + dg * 4:16 + dg * 4 + 4, :, tok0:tok0 + TB3].rearrange("k p n -> p k n")), sga, dg, tok0),
                    T_sg[sga_i], writes=(T_sg[sga_i],))
                sgb_i = sgi[0] % 2
                sgi[0] += 1
                sgb = sg[sgb_i].rearrange("p (k t) -> p k t", k=4)
                P.dma("sp", mk(lambda e, sgb, dg, tok0: e.dma_start(
                    out=sgb, in_=pb[32 + dg * 4:32 + dg * 4 + 4, :, tok0:tok0 + TB3].rearrange("k p n -> p k n")), sgb, dg, tok0),
                    T_sg[sgb_i], writes=(T_sg[sgb_i],))
                for cc in range(4):
                    dc = dg * 4 + cc
                    ba = rot.next()
                    bb = rot.next()
                    for k in range(8):
                        P.op("pe", mk(lambda e, ba, wa3, k, cc: e.matmul(banks[ba][:], wa3[:, k, cc * 128:(cc + 1) * 128], og3[:, k, :],
                                                                        start=(k == 0), stop=(k == 7)), ba, wa3, k, cc),
                             reads=Twa + (T_og,), writes=(bank_t[ba],), merge=(k > 0))
                    for k in range(8):
                        P.op("pe", mk(lambda e, bb, wb3, k, cc: e.matmul(banks[bb][:], wb3[:, k, cc * 128:(cc + 1) * 128], z3[:, k, :],
                                                                        start=(k == 0), stop=(k == 7)), bb, wb3, k, cc),
                             reads=Twb + (T_z,), writes=(bank_t[bb],), merge=(k > 0))
                    t1i = tmi[0] % 3
                    t2i = (tmi[0] + 1) % 3
                    tmi[0] += 2
                    P.op("dve", mk(lambda e, t1i, ba, sga, cc: e.tensor_mul(tmp[t1i], banks[ba][:], sga[:, cc, :]), t1i, ba, sga, cc),
                         reads=(bank_t[ba], T_sg[sga_i]), writes=(T_tmp[t1i],))
                    P.op("dve", mk(lambda e, t2i, bb, sgb, cc: e.tensor_mul(tmp[t2i], banks[bb][:], sgb[:, cc, :]), t2i, bb, sgb, cc),
                         reads=(bank_t[bb], T_sg[sgb_i]), writes=(T_tmp[t2i],))
                    P.op("pool", mk(lambda e, dc, t1i, t2i: e.tensor_add(mh3[:, dc, :], tmp[t1i], tmp[t2i]), dc, t1i, t2i),
                         reads=(T_tmp[t1i], T_tmp[t2i]), writes=(T_mh,), merge=(dc > 0))
            for db in range(4):
                w3, Tw = wload(wb_out[l], 0, KC, db * 512, ("out", l))
                for tt in range(4):
                    b = rot.next()
                    for k in range(KC):
                        P.op("pe", mk(lambda e, b, k, tt, w3: e.matmul(banks[b][:], mh3[:, k, tt * 128:(tt + 1) * 128], w3[:, k, :],
                                                                      start=(k == 0), stop=(k == KC - 1)), b, k, tt, w3),
                             reads=(T_mh,) + Tw, writes=(bank_t[b],), merge=(k > 0))
                    ti = tmi[0] % 3
                    tmi[0] += 1
                    P.op("dve", mk(lambda e, ti, b, db: e.tensor_mul(tmp[ti], banks[b][:], gb1[:, db * 512:(db + 1) * 512]), ti, b, db),
                         reads=(bank_t[b], T_gb), writes=(T_tmp[ti],))
                    P.op("pool", mk(lambda e, ti, tt, db: e.tensor_add(xt[tt][:, db * 512:(db + 1) * 512],
                                                                      xt[tt][:, db * 512:(db + 1) * 512], tmp[ti]), ti, tt, db),
                         reads=(T_tmp[ti], T_xt[tt]), writes=(T_xt[tt],))
            for tt in range(4):
                i = tt % 2
                norm_T(xt[tt], T_xt[tt], xn[i], T_xn[i], junk, T_junk, ss[i], T_ss[i], mh3, T_mh, tt * 128,
                       v_sc2[l], v_sh2[l], first=(tt == 0))
            for jg in range(22):
                wA, TwA = wload(wb_ffi[l], 0, KC, jg * 256, ("ffi", l), ncol=256)
                wG, TwG = wload(wb_ffi[l], 0, KC, D_FF + jg * 256, ("ffi", l), ncol=256)
                for cc in range(2):
                    fc = jg * 2 + cc
                    ba = rot.next()
                    bb = rot.next()
                    for k in range(KC):
                        P.op("pe", mk(lambda e, ba, wA, k, cc: e.matmul(banks[ba][:], wA[:, k, cc * 128:(cc + 1) * 128], mh3[:, k, :],
                                                                       start=(k == 0), stop=(k == KC - 1)), ba, wA, k, cc),
                             reads=TwA + (T_mh,), writes=(bank_t[ba],), merge=(k > 0))
                    for k in range(KC):
                        P.op("pe", mk(lambda e, bb, wG, k, cc: e.matmul(banks[bb][:], wG[:, k, cc * 128:(cc + 1) * 128], mh3[:, k, :],
                                                                       start=(k == 0), stop=(k == KC - 1)), bb, wG, k, cc),
                             reads=TwG + (T_mh,), writes=(bank_t[bb],), merge=(k > 0))
                    ti = tmi[0] % 3
                    tmi[0] += 1
                    P.op("act", mk(lambda e, ti, ba: e.activation(out=tmp[ti], in_=banks[ba][:], func=AF.Silu), ti, ba),
                         reads=(bank_t[ba],), writes=(T_tmp[ti],))
                    P.op("dve", mk(lambda e, ti, bb, fc: e.tensor_mul(act3[:, fc, :], tmp[ti], banks[bb][:]), ti, bb, fc),
                         reads=(T_tmp[ti], bank_t[bb]), writes=(T_act,), merge=(fc > 0))
            for db in range(4):
                bks = [rot.next() for _ in range(4)]
                for kg in range(4):
                    w3, Tw = wload(wb_ffo[l], kg * 11 * 128, 11, db * 512, ("ffo", l))
                    for tt in range(4):
                        b = bks[tt]
                        for k in range(11):
                            kk = kg * 11 + k
                            P.op("pe", mk(lambda e, b, kk, k, tt, w3: e.matmul(banks[b][:], act3[:, kk, tt * 128:(tt + 1) * 128], w3[:, k, :],
                                                                              start=(kk == 0), stop=(kk == FC - 1)), b, kk, k, tt, w3),
                                 reads=(T_act,) + Tw, writes=(bank_t[b],), merge=(kk > 0))
                for tt in range(4):
                    b = bks[tt]
                    ti = tmi[0] % 3
                    tmi[0] += 1
                    P.op("dve", mk(lambda e, ti, b, db: e.tensor_mul(tmp[ti], banks[b][:], gb2[:, db * 512:(db + 1) * 512]), ti, b, db),
                         reads=(bank_t[b], T_gb), writes=(T_tmp[ti],))
                    P.op("pool", mk(lambda e, ti, tt, db: e.tensor_add(xt[tt][:, db * 512:(db + 1) * 512],
                                                                      xt[tt][:, db * 512:(db + 1) * 512], tmp[ti]), ti, tt, db),
                         reads=(T_tmp[ti], T_xt[tt]), writes=(T_xt[tt],))
            for tt in range(4):
                r0 = tok0 + tt * 128
                if not last:
                    P.dma("pool", mk(lambda e, tt, r0: e.dma_start(out=xs[r0:r0 + 128, :], in_=xt[tt]), tt, r0),
                          T_xt[tt], reads=(T_xt[tt],), writes=(T_x[r0 // 128],))
                else:
                    i = tt % 2
                    P.op("act", mk(lambda e, tt, i: e.activation(out=junk, in_=xt[tt], func=AF.Square, accum_out=ss[i][:, 0:1]), tt, i),
                         reads=(T_xt[tt],), writes=(T_junk, T_ss[i]))
                    P.op("act", mk(lambda e, i: e.activation(out=ss[i][:, 1:2], in_=ss[i][:, 0:1], func=AF.Sqrt, scale=1.0 / D, bias=c_eps), i),
                         reads=(T_ss[i], T_const), writes=(T_ss[i],))
                    P.op("dve", mk(lambda e, i: e.reciprocal(ss[i][:, 2:3], ss[i][:, 1:2]), i), reads=(T_ss[i],), writes=(T_ss[i],))
                    P.op("dve", mk(lambda e, tt, i: e.scalar_tensor_tensor(xt[tt], xt[tt], ss[i][:, 2:3], nfb, ALU.mult, ALU.mult), tt, i),
                         reads=(T_xt[tt], T_ss[i], T_gb), writes=(T_xt[tt],))
                    P.dma("pool", mk(lambda e, tt, r0: e.dma_start(out=y_out[r0:r0 + 128, :], in_=xt[tt]), tt, r0),
                          T_xt[tt], reads=(T_xt[tt],))
        P.barrier()

    import os
    nph = int(os.environ.get("KPH", "99"))
    ph = 0
    for l in range(DEPTH):
        for f in (phase_P1, phase_P2a, phase_P2b, phase_P3):
            if ph < nph:
                f(l)
            ph += 1
    P.emit()
    return nc, stack


GRID_W = 64
POS_BASE = 10000.0


def _pos_embed(n):
    rows = n // GRID_W
    r, col = np.meshgrid(np.arange(rows, dtype=np.float32), np.arange(GRID_W, dtype=np.float32), indexing="ij")
    quarter = D // 4
    omega = (1.0 / (np.float32(POS_BASE) ** (np.arange(quarter, dtype=np.float32) / np.float32(quarter)))).astype(np.float32)

    def enc(p):
        a = (p.reshape(-1)[:, None] * omega[None, :]).astype(np.float32)
        return np.concatenate([np.sin(a), np.cos(a)], axis=-1)
    return np.concatenate([enc(r), enc(col)], axis=-1).astype(np.float32)


def _dft_tables(nt, nper):
    n = np.arange(nper)
    ang = 2.0 * np.pi * ((np.outer(n, n) % nper).astype(np.float64)) / nper
    c = np.cos(ang) / np.sqrt(nper)
    s = -np.sin(ang) / np.sqrt(nper)
    out = np.zeros((2, nt, nt), dtype=ml_dtypes.bfloat16)
    for b in range(nt // nper):
        out[0, b * nper:(b + 1) * nper, b * nper:(b + 1) * nper] = c.astype(ml_dtypes.bfloat16)
        out[1, b * nper:(b + 1) * nper, b * nper:(b + 1) * nper] = s.astype(ml_dtypes.bfloat16)
    return out


def _consts():
    n = np.arange(256)
    ang = 2.0 * np.pi * ((np.outer(n, n) % 256).astype(np.float64)) / 256
    cfsf = np.concatenate([np.cos(ang), np.sin(ang)], axis=1) / 16.0
    t = np.arange(128)
    same = (t[:, None] // CH) == (t[None, :] // CH)
    mf = (same & (t[:, None] <= t[None, :])).astype(np.float32)
    mb = (same & (t[:, None] >= t[None, :])).astype(np.float32)
    ident = np.eye(128, dtype=np.float32)
    cm = np.ones((128, 128), np.float32)
    cm[:, ::CH] = 0.0
    rm = np.zeros((128, 128), np.float32)
    for c in range(4):
        rm[c * 32:(c + 1) * 32, c] = 1.0
    return dict(cfsf=cfsf.astype(ml_dtypes.bfloat16), masks=np.stack([mf, mb, ident, rm]), cmask=cm)


WEIGHT_KEYS = ["w_mod", "b_mod", "norm_mix", "norm_ffn", "w_in", "lb_raw", "g_norm", "w_a", "w_b", "w_out",
               "w_ff_in", "w_ff_out", "norm_final"]


_CACHE = {}


def kernel(x_prompt, x_sample, state_hgrn, c, c_ctx, w_mod, b_mod, norm_mix, norm_ffn, w_in, lb_raw,
           g_norm, w_a, w_b, w_out, w_ff_in, w_ff_out, norm_final):
    NSEG = 16
    NT = NSEG * SEG
    f32 = lambda a: np.ascontiguousarray(np.asarray(a, dtype=np.float32))
    if "nc" not in _CACHE:
        _CACHE["nc"] = build(NSEG)
    nc, _stack = _CACHE["nc"]
    w = dict(w_mod=f32(w_mod), b_mod=f32(b_mod), norm_mix=f32(norm_mix), norm_ffn=f32(norm_ffn), w_in=f32(w_in),
             lb_raw=f32(lb_raw), g_norm=f32(g_norm), w_a=f32(w_a), w_b=f32(w_b), w_out=f32(w_out),
             w_ff_in=f32(w_ff_in), w_ff_out=f32(w_ff_out), norm_final=f32(norm_final))
    cs = _consts()
    pe = _pos_embed(NT)
    dn_full = _dft_tables(NT, NT)
    dn_seg = _dft_tables(NT, SEG)
    x_prompt = f32(x_prompt)
    x_sample = f32(x_sample)
    state_hgrn = f32(state_hgrn)
    c = f32(c)
    c_ctx = f32(c_ctx)
    in_maps = []
    for core in range(8):
        m = dict(w)
        m.update(cs)
        if core < 4:
            m.update(xin=x_sample[core], pe=pe, cvec=c[core], s0=np.ascontiguousarray(state_hgrn[core]),
                     carry=np.ones((128, 1), np.float32), dn=dn_full)
        else:
            j = core - 4
            xin = np.zeros((NT, D), np.float32)
            xin[:8 * SEG] = x_prompt[8 * j:8 * j + 8].reshape(8 * SEG, D)
            m.update(xin=xin, pe=np.zeros((NT, D), np.float32), cvec=c_ctx,
                     s0=np.zeros((DEPTH, 2, H_A, 128, 128), np.float32),
                     carry=np.zeros((128, 1), np.float32), dn=dn_seg)
        in_maps.append(m)
    res = run_bass_kernel_spmd(nc, in_maps, core_ids=list(range(8)))
    r = res.results
    y_sample = np.stack([r[b]["y"] for b in range(4)], axis=0).astype(np.float32)
    y_prompt = np.concatenate([r[4 + j]["y"][:8 * SEG].reshape(8, SEG, D) for j in range(4)], axis=0).astype(np.float32)
    st = np.zeros((32, DEPTH, 2, H_A, 128, 128), np.float32)
    for j in range(4):
        s = r[4 + j]["st"]
        st[8 * j:8 * j + 8] = np.transpose(s[:, :8], (1, 0, 2, 3, 4, 5))
    return (y_prompt, y_sample, st)
```

```python
import contextlib
import numpy as np
import ml_dtypes
import concourse.bass as bass
import concourse.mybir as mybir
from concourse.bass_utils import run_bass_kernel_spmd

F32 = mybir.dt.float32
BF16 = mybir.dt.bfloat16
AF = mybir.ActivationFunctionType
ALU = mybir.AluOpType
AX = mybir.AxisListType

D = 2048
KC = 16
H_A = 8
W_A = 1024
W_B = 1024
N_IN = 10240
D_FF = 5632
FC = 44
EPS = 1e-6
CH = 32
DEPTH = 2
SEG = 256


class Tl:
    __slots__ = ("name", "w", "r", "s", "pr")

    def __init__(self, name):
        self.name = name
        self.w = {}
        self.r = {}
        self.pr = {}
        self.s = {}


def _addref(d, ref):
    if ref[0] == "e":
        if d.get(ref[1], -1) < ref[2]:
            d[ref[1]] = ref[2]
    else:
        d[id(ref[1])] = ref[1]


class Op:
    __slots__ = ("eng", "idx", "fn", "edeps", "ddeps", "sig", "dma", "val")

    def __init__(self, eng, idx, fn):
        self.eng = eng
        self.idx = idx
        self.fn = fn
        self.edeps = {}
        self.ddeps = {}
        self.sig = False
        self.dma = None
        self.val = 0


COMPUTE = ("pe", "act", "dve", "pool")
QUEUES = ("sp", "act", "pool")


class Prog:
    def __init__(self, nc, stack):
        self.nc = nc
        self.stack = stack
        self.ops = {e: [] for e in ("pe", "act", "dve", "pool", "sp")}
        self.known_e = {e: {} for e in self.ops}
        self.known_d = {e: {} for e in self.ops}
        self.dsems = []
        self.free = {"hw": [], "sw": []}
        self.allsems = {}
        self.nsem = 0
        self._cap = None

    def capture(self):
        self._cap = []

    def end_capture(self):
        c, self._cap = self._cap, None
        return c

    def replay(self, lists):
        n = max(len(x) for x in lists)
        for i in range(n):
            for x in lists:
                if i < len(x):
                    kind, args, kw = x[i]
                    (self.op if kind == "op" else self.dma)(*args, **kw)

    def tile(self, name):
        return Tl(name)

    def _dsem(self, t, kind):
        if kind not in t.s:
            if self.free[kind]:
                sem, base = self.free[kind].pop()
            else:
                sem, base = self.stack.enter_context(self.nc.semaphore("d%d" % self.nsem)), 0
                self.nsem += 1
            t.s[kind] = [sem, base]
            for e in self.known_d:
                self.known_d[e][(id(t), kind)] = base
            if t not in self.dsems:
                self.dsems.append(t)
        return t.s[kind]

    def _collect(self, op, reads, writes, merge):
        eng = op.eng

        def add(refs, skip_same):
            for k, v in refs.items():
                if isinstance(k, str):
                    e2, i2 = k, v
                    if e2 == eng and (skip_same or eng == "pe" or eng == "sp"):
                        continue
                    if self.known_e[eng].get(e2, -1) >= i2:
                        continue
                    if op.edeps.get(e2, -1) < i2:
                        op.edeps[e2] = i2
                else:
                    t = v
                    for kind, (sem_, c) in t.s.items():
                        if self.known_d[eng].get((id(t), kind), 0) >= c:
                            continue
                        op.ddeps[(id(t), kind)] = (sem_, c)

        for t in reads:
            add(t.w, False)
        for t in writes:
            if not merge:
                add(t.w, True)
            else:
                add(t.pr, True)
            add(t.r, True)
        for e2, i2 in op.edeps.items():
            self.known_e[eng][e2] = i2
        for k, (sem_, v) in op.ddeps.items():
            self.known_d[eng][k] = v

    def op(self, eng, fn, reads=(), writes=(), merge=False):
        if self._cap is not None:
            self._cap.append(("op", (eng, fn), dict(reads=reads, writes=writes, merge=merge)))
            return None
        op = Op(eng, len(self.ops[eng]), fn)
        self._collect(op, reads, writes, merge)
        self.ops[eng].append(op)
        ref = ("e", eng, op.idx)
        for t in reads:
            _addref(t.r, ref)
        for t in writes:
            if not merge:
                t.w = {}
                t.pr = t.r
                t.r = {}
            _addref(t.w, ref)
        return op

    def dma(self, q, fn, semtile, reads=(), writes=(), merge=False):
        if self._cap is not None:
            self._cap.append(("dma", (q, fn, semtile), dict(reads=reads, writes=writes, merge=merge)))
            return None
        op = Op(q, len(self.ops[q]), fn)
        self._collect(op, reads, writes, merge)
        rec = self._dsem(semtile, "sw" if q == "pool" else "hw")
        rec[1] += 16
        op.dma = (rec[0], semtile)
        self.allsems[id(rec[0])] = [rec[0], rec[1]]
        self.ops[q].append(op)
        ref = ("d", semtile)
        for t in reads:
            _addref(t.r, ref)
        for t in writes:
            if not merge:
                t.w = {}
                t.pr = t.r
                t.r = {}
            _addref(t.w, ref)
        return op

    def barrier(self):
        last = {}
        for e in COMPUTE:
            i = len(self.ops[e]) - 1
            while i >= 0 and (self.ops[e][i].fn is None or self.ops[e][i].dma is not None):
                i -= 1
            last[e] = i
        for e in self.ops:
            op = Op(e, len(self.ops[e]), None)
            for e2, i2 in last.items():
                if e2 != e and i2 >= 0 and self.known_e[e].get(e2, -1) < i2:
                    op.edeps[e2] = i2
                    self.known_e[e][e2] = i2
            for t in self.dsems:
                for kind, (sem_, c) in t.s.items():
                    if self.known_d[e].get((id(t), kind), 0) < c:
                        op.ddeps[(id(t), kind)] = (sem_, c)
                        self.known_d[e][(id(t), kind)] = c
            self.ops[e].append(op)
        for t in self.dsems:
            for kind, (sem_, c) in t.s.items():
                self.free[kind].append((sem_, c))
            t.s = {}
        self.dsems = []

    def emit(self):
        nc = self.nc
        esem = {e: self.stack.enter_context(nc.semaphore("e_" + e)) for e in COMPUTE}
        for e in self.ops:
            for op in self.ops[e]:
                for e2, i2 in op.edeps.items():
                    self.ops[e2][i2].sig = True
        for e in COMPUTE:
            c = 0
            for op in self.ops[e]:
                if op.sig:
                    assert op.fn is not None and op.dma is None, "bad signalling op"
                    c += 1
                op.val = c
        final = list(self.allsems.values())

        def run(ename, eng):
            for op in self.ops[ename]:
                for e2, i2 in op.edeps.items():
                    eng.wait_ge(esem[e2], self.ops[e2][i2].val)
                for k, (sem_, v) in op.ddeps.items():
                    eng.wait_ge(sem_, v)
                if op.fn is None:
                    continue
                ins = op.fn(eng)
                if op.dma is not None:
                    ins.then_inc(op.dma[0], 16)
                elif op.sig:
                    ins.then_inc(esem[ename], 1)
            if ename == "sp":
                for sem_, v in final:
                    eng.wait_ge(sem_, v)

        with nc.Block() as block:
            @block.sync
            def _(e):
                run("sp", e)

            @block.scalar
            def _(e):
                run("act", e)

            @block.vector
            def _(e):
                run("dve", e)

            @block.gpsimd
            def _(e):
                run("pool", e)

            @block.tensor
            def _(e):
                run("pe", e)


class Sb:
    def __init__(self, big, nwords):
        self.big = big
        self.n = nwords
        self.off = 0

    def reset(self, off=0):
        self.off = off

    def f32(self, cols):
        a = self.off
        self.off += cols
        assert self.off <= self.n, ("SBUF overflow", self.off, self.n)
        return self.big[:, a:a + cols]

    def bf16(self, cols):
        w = (cols + 1) // 2
        a = self.off
        self.off += w
        assert self.off <= self.n, ("SBUF overflow", self.off, self.n)
        return self.big[:, a:a + w].bitcast(BF16)


def build(NSEG, debug=False):
    NT = NSEG * SEG
    NTT = NT // 128
    TB = min(1024, NT)
    NB = NT // TB
    TBT = TB // 128
    NH5 = TB // 512
    NCHK = NT // CH

    import os
    nc = bass.Bass("TRN2", target_bir_lowering=False)
    stack = contextlib.ExitStack()
    P = Prog(nc, stack)

    def din(name, shape, dt=F32):
        return nc.dram_tensor(name, list(shape), dt, kind="ExternalInput").ap()

    def dscr(name, shape, dt=F32, out=False):
        kind = "ExternalOutput" if (out or debug) else "Internal"
        return nc.dram_tensor(name, list(shape), dt, kind=kind).ap()

    xin = din("xin", [NT, D])
    pe_in = din("pe", [NT, D])
    cvec = din("cvec", [D])
    s0 = din("s0", [DEPTH, 2, H_A, 128, 128])
    carry_in = din("carry", [128, 1])
    dn = din("dn", [2, NT, NT], BF16)
    cfsf = din("cfsf", [256, 512], BF16)
    masks = din("masks", [4, 128, 128])
    cmask = din("cmask", [128, 128])
    w_mod = din("w_mod", [DEPTH, D, 6 * D])
    b_mod = din("b_mod", [DEPTH, 6 * D])
    norm_mix = din("norm_mix", [DEPTH, D])
    norm_ffn = din("norm_ffn", [DEPTH, D])
    w_in = din("w_in", [DEPTH, D, N_IN])
    lb_raw = din("lb_raw", [DEPTH, 2, W_A])
    g_norm = din("g_norm", [DEPTH, 128])
    w_a = din("w_a", [DEPTH, W_A, D])
    w_b = din("w_b", [DEPTH, W_B, D])
    w_out = din("w_out", [DEPTH, D, D])
    w_ff_in = din("w_ff_in", [DEPTH, D, 2 * D_FF])
    w_ff_out = din("w_ff_out", [DEPTH, D_FF, D])
    norm_final = din("norm_final", [D])
    y_out = nc.dram_tensor("y", [NT, D], F32, kind="ExternalOutput").ap()
    st_out = nc.dram_tensor("st", [DEPTH, NSEG, 2, H_A, 128, 128], F32, kind="ExternalOutput").ap()
    wb_in = dscr("wb_in", [DEPTH, D, N_IN], BF16)
    wb_a = dscr("wb_a", [DEPTH, W_A, D], BF16)
    wb_b = dscr("wb_b", [DEPTH, W_B, D], BF16)
    wb_out = dscr("wb_out", [DEPTH, D, D], BF16)
    wb_ffi = dscr("wb_ffi", [DEPTH, D, 2 * D_FF], BF16)
    wb_ffo = dscr("wb_ffo", [DEPTH, D_FF, D], BF16)
    xs = dscr("xs", [NT, D])
    pq = dscr("pq", [24, 128, NT])
    pb = dscr("pb", [48, 128, NT], BF16)
    vv = dscr("vv", [NT, W_A], BF16)
    og_s = dscr("ogs", [8, 128, NT], BF16)
    z_s = dscr("zs", [8, 128, NT], BF16)
    g_s = dscr("gs", [DEPTH, 2, D])

    big_h = stack.enter_context(nc.sbuf_tensor("big", [128, 49152], F32))
    SB = Sb(big_h, 49152)
    banks = [stack.enter_context(nc.psum_tensor("ps%d" % i, [128, 512], F32)) for i in range(8)]
    bank_t = [P.tile("bank%d" % i) for i in range(8)]

    c_ident = SB.f32(128)
    c_mf = SB.f32(128)
    c_mb = SB.f32(128)
    c_cm = SB.f32(128)
    c_identb = SB.bf16(128)
    c_ones = SB.f32(128)
    c_rm = SB.f32(128)
    c_carry = SB.f32(1)
    c_eps = SB.f32(1)
    v_sc1 = [SB.f32(KC) for _ in range(DEPTH)]
    v_sh1 = [SB.f32(KC) for _ in range(DEPTH)]
    v_sc2 = [SB.f32(KC) for _ in range(DEPTH)]
    v_sh2 = [SB.f32(KC) for _ in range(DEPTH)]
    v_lb = SB.f32(16)
    v_oml = SB.f32(16)
    v_gn = SB.f32(DEPTH)
    T_const = P.tile("const")
    T_vec = P.tile("vecs")
    PERSIST = SB.off

    def ld(q, dst, src, tl, reads=(), extra_w=()):
        P.dma(q, lambda e: e.dma_start(out=dst, in_=src), tl, reads=reads, writes=(tl,) + tuple(extra_w))

    for i, (dst, src) in enumerate([(c_ident, masks[2]), (c_mf, masks[0]), (c_mb, masks[1]), (c_cm, cmask), (c_rm, masks[3]),
                                    (c_carry, carry_in)]):
        P.dma("sp", (lambda d_, s_: (lambda e: e.dma_start(out=d_, in_=s_)))(dst, src), T_const,
              writes=(T_const,), merge=True)
    P.op("dve", lambda e: e.tensor_copy(c_identb, c_ident), reads=(T_const,), writes=(T_const,), merge=True)
    P.op("dve", lambda e: e.memset(c_ones, 1.0), writes=(T_const,), merge=True)
    P.op("dve", lambda e: e.memset(c_eps, EPS), writes=(T_const,), merge=True)

    T_w = {}
    NST = 4

    def conv_gen(items, stg_f, stg_b, T_sf, T_sb, ldq, stq, engs):
        tiles = []
        table = {"in": (w_in, wb_in, D, N_IN), "a": (w_a, wb_a, W_A, D), "b": (w_b, wb_b, W_B, D),
                 "out": (w_out, wb_out, D, D), "ffi": (w_ff_in, wb_ffi, D, 2 * D_FF), "ffo": (w_ff_out, wb_ffo, D_FF, D)}
        for l_, nm_ in items:
            src_, dst_, rows, cols = table[nm_]
            src, dst, key = src_[l_], dst_[l_], (nm_, l_)
            T_w[key] = P.tile("w_" + str(key))
            for r0 in range(0, rows, 128):
                for c0 in range(0, cols, 2048):
                    cw = min(2048, cols - c0)
                    tiles.append((src[r0:r0 + 128, c0:c0 + cw], dst[r0:r0 + 128, c0:c0 + cw], cw, key))
        LA = 2

        def load(i):
            s_ap, d_ap, cw, key = tiles[i]
            b = i % NST
            sf = stg_f[b][:, :cw]
            P.dma(ldq, lambda e: e.dma_start(out=sf, in_=s_ap), T_sf[b], writes=(T_sf[b],))

        def finish(i):
            s_ap, d_ap, cw, key = tiles[i]
            b = i % NST
            sf, sb_ = stg_f[b][:, :cw], stg_b[b][:, :cw]
            ce = engs[i % len(engs)]
            if ce == "act":
                P.op("act", lambda e: e.copy(sb_, sf), reads=(T_sf[b],), writes=(T_sb[b],))
            else:
                P.op(ce, lambda e: e.tensor_copy(sb_, sf), reads=(T_sf[b],), writes=(T_sb[b],))
            P.dma(stq, lambda e: e.dma_start(out=d_ap, in_=sb_), T_sb[b], reads=(T_sb[b],), writes=(T_w[key],), merge=True)

        for i in range(len(tiles) + LA):
            if i < len(tiles):
                load(i)
            if i >= LA:
                finish(i - LA)
                yield

    SB.reset(PERSIST)
    stg_f0 = [SB.f32(2048) for _ in range(NST)]
    stg_b0 = [SB.bf16(2048) for _ in range(NST)]
    T_sf0 = [P.tile("sf%d" % i) for i in range(NST)]
    T_sb0 = [P.tile("sb%d" % i) for i in range(NST)]
    BGCONV = os.environ.get("BGCONV", "1") == "1"
    REST = ["a", "b", "out", "ffi", "ffo"]
    if BGCONV:
        up_items = [(0, "in")]
    else:
        up_items = [(l_, n_) for l_ in range(DEPTH) for n_ in ["in"] + REST]
    for _ in conv_gen(up_items, stg_f0, stg_b0, T_sf0, T_sb0, "sp", "pool", ["dve", "act", "pool"]):
        pass
    P.barrier()

    SB.reset(PERSIST)
    m_ccol = SB.f32(KC)
    m_scol = SB.f32(KC)
    m_rep = SB.f32(KC * 128)
    m_w = [SB.f32(KC * 512) for _ in range(2)]
    m_b = [SB.f32(512) for _ in range(2)]
    m_row = SB.f32(512)
    m_tmp = SB.f32(128)
    m_lb = SB.f32(32)
    m_nm = SB.f32(2 * KC)
    m_nf = SB.f32(2 * KC)
    T_mc = P.tile("mc")
    T_mw = [P.tile("mw%d" % i) for i in range(2)]
    T_mb = [P.tile("mb%d" % i) for i in range(2)]
    T_mrow = P.tile("mrow")
    T_mtmp = P.tile("mtmp")
    T_mlb = P.tile("mlb")
    T_mn = P.tile("mn")
    T_gs = P.tile("gs")

    P.dma("sp", lambda e: e.dma_start(out=m_ccol, in_=cvec.rearrange("(k p) -> p k", p=128),
                                      allow_slow_non_contiguous=True), T_mc, writes=(T_mc,))
    P.op("act", lambda e: e.activation(out=m_scol, in_=m_ccol, func=AF.Silu), reads=(T_mc,), writes=(T_mc,))
    m_rep3 = m_rep.rearrange("p (k m) -> p k m", m=128)
    P.op("dve", lambda e: e.tensor_copy(m_rep3, m_scol.unsqueeze(2).to_broadcast([128, KC, 128])),
         reads=(T_mc,), writes=(T_mc,))
    P.dma("sp", lambda e: e.dma_start(out=m_nm.rearrange("p (l k) -> p l k", l=2),
                                      in_=norm_mix.rearrange("l (k p) -> p l k", p=128),
                                      allow_slow_non_contiguous=True), T_mn, writes=(T_mn,))
    P.dma("sp", lambda e: e.dma_start(out=m_nf.rearrange("p (l k) -> p l k", l=2),
                                      in_=norm_ffn.rearrange("l (k p) -> p l k", p=128),
                                      allow_slow_non_contiguous=True), T_mn, writes=(T_mn,), merge=True)
    P.dma("sp", lambda e: e.dma_start(out=v_gn, in_=g_norm.rearrange("l p -> p l"),
                                      allow_slow_non_contiguous=True), T_vec, writes=(T_vec,), merge=True)
    P.dma("sp", lambda e: e.dma_start(out=m_lb.rearrange("p (l r h) -> p l r h", l=2, r=2),
                                      in_=lb_raw.rearrange("l r (h p) -> p l r h", p=128),
                                      allow_slow_non_contiguous=True), T_mlb, writes=(T_mlb,))
    P.op("dve", lambda e: e.tensor_sub(m_lb[:, 0:16], m_lb[:, 16:32], m_lb[:, 0:16]), reads=(T_mlb,), writes=(T_mlb,))
    P.op("act", lambda e: e.activation(out=v_lb, in_=m_lb[:, 0:16], func=AF.Sigmoid), reads=(T_mlb,),
         writes=(T_vec,), merge=True)
    P.op("dve", lambda e: e.tensor_scalar(v_oml, v_lb, -1.0, 1.0, ALU.mult, ALU.add), reads=(T_vec,),
         writes=(T_vec,), merge=True)

    mi = 0
    for l in range(DEPTH):
        for j in range(6):
            for nb in range(4):
                i = mi % 2
                mi += 1
                n0 = j * D + nb * 512
                P.dma("sp", (lambda a, l_, n_: (lambda e: e.dma_start(
                    out=a.rearrange("p (k n) -> p k n", n=512),
                    in_=w_mod[l_, :, n_:n_ + 512].rearrange("(k p) n -> p k n", p=128))))(m_w[i], l, n0),
                    T_mw[i], writes=(T_mw[i],))
                P.dma("sp", (lambda a, l_, n_: (lambda e: e.dma_start(
                    out=a, in_=b_mod[l_, n_:n_ + 512].partition_broadcast(128))))(m_b[i], l, n0),
                    T_mb[i], writes=(T_mb[i],))
                bk = banks[i][:]
                w3 = m_w[i].rearrange("p (k n) -> p k n", n=512)
                for k in range(KC):
                    P.op("pe", (lambda b_, k_, w_: (lambda e: e.matmul(b_, m_rep3[:, k_, :], w_[:, k_, :],
                                                                      start=(k_ == 0), stop=(k_ == KC - 1))))(bk, k, w3),
                         reads=(T_mc, T_mw[i]), writes=(bank_t[i],), merge=(k > 0))
                if j in (2, 5):
                    P.op("dve", (lambda b_, mb_: (lambda e: e.tensor_add(m_row, b_, mb_)))(bk, m_b[i]),
                         reads=(bank_t[i], T_mb[i]), writes=(T_mrow,))
                    P.dma("sp", (lambda l_, j_, nb_: (lambda e: e.dma_start(
                        out=g_s[l_, j_ // 3, nb_ * 512:(nb_ + 1) * 512], in_=m_row[0:1, :])))(l, j, nb),
                        T_mrow, reads=(T_mrow,), writes=(T_gs,), merge=True)
                else:
                    P.op("dve", (lambda b_, mb_: (lambda e: e.tensor_add(m_row, b_, mb_)))(bk, m_b[i]),
                         reads=(bank_t[i], T_mb[i]), writes=(T_mrow,))
                    if j in (1, 4):
                        P.op("dve", lambda e: e.tensor_scalar_add(m_row, m_row, 1.0), reads=(T_mrow,), writes=(T_mrow,))
                    vdst = {0: v_sh1, 1: v_sc1, 3: v_sh2, 4: v_sc2}[j][l]
                    for c in range(4):
                        kc = nb * 4 + c
                        P.op("dve", (lambda c_: (lambda e: e.tensor_mul(m_tmp, m_row[:, c_ * 128:(c_ + 1) * 128], c_ident)))(c),
                             reads=(T_mrow, T_const), writes=(T_mtmp,))
                        P.op("dve", (lambda v_, kc_: (lambda e: e.reduce_sum(v_[:, kc_:kc_ + 1], m_tmp, AX.X)))(vdst, kc),
                             reads=(T_mtmp,), writes=(T_vec,), merge=True)
        P.op("dve", (lambda l_: (lambda e: e.tensor_mul(v_sc1[l_], v_sc1[l_], m_nm[:, l_ * KC:(l_ + 1) * KC])))(l),
             reads=(T_vec, T_mn), writes=(T_vec,), merge=True)
        P.op("dve", (lambda l_: (lambda e: e.tensor_mul(v_sc2[l_], v_sc2[l_], m_nf[:, l_ * KC:(l_ + 1) * KC])))(l),
             reads=(T_vec, T_mn), writes=(T_vec,), merge=True)
    P.barrier()

    def mk(f, *a):
        return lambda e: f(e, *a)

    bankbf = [b[:].bitcast(BF16) for b in banks]
    T_x = [P.tile("x%d" % i) for i in range(NTT)]
    T_scr = {}

    def dtl(*key):
        if key not in T_scr:
            T_scr[key] = P.tile(str(key))
        return T_scr[key]

    class Rot:
        def __init__(self, ids):
            self.ids = list(ids)
            self.i = 0

        def next(self):
            b = self.ids[self.i % len(self.ids)]
            self.i += 1
            return b

    def norm_T(xt_ap, T_xt, xn_ap, T_xn, junk_ap, T_junk, ss_ap, T_ss, hT3, T_hT, col0, scale_col, shift_col, first):
        P.op("act", lambda e: e.activation(out=junk_ap, in_=xt_ap, func=AF.Square, accum_out=ss_ap[:, 0:1]),
             reads=(T_xt,), writes=(T_junk, T_ss))
        P.op("act", lambda e: e.activation(out=ss_ap[:, 1:2], in_=ss_ap[:, 0:1], func=AF.Sqrt, scale=1.0 / D, bias=c_eps),
             reads=(T_ss, T_const), writes=(T_ss,))
        P.op("dve", lambda e: e.reciprocal(ss_ap[:, 2:3], ss_ap[:, 1:2]), reads=(T_ss,), writes=(T_ss,))
        P.op("dve", lambda e: e.tensor_scalar_mul(xn_ap, xt_ap, ss_ap[:, 2:3]), reads=(T_xt, T_ss), writes=(T_xn,))
        for half in range(2):
            b = half
            for j in range(8):
                kc = half * 8 + j
                P.op("pe", mk(lambda e, kc, j, b: e.transpose(bankbf[b][:, j * 128:(j + 1) * 128],
                                                              xn_ap[:, kc * 128:(kc + 1) * 128], c_identb), kc, j, b),
                     reads=(T_xn, T_const), writes=(bank_t[b],), merge=(j > 0))
            for j in range(8):
                kc = half * 8 + j
                dst = hT3[:, kc, col0:col0 + 128]
                src = bankbf[b][:, j * 128:(j + 1) * 128]
                mg = not (first and half == 0 and j == 0)
                if half == 0:
                    P.op("act", mk(lambda e, d_, s_, kc: e.activation(out=d_, in_=s_, func=AF.Identity,
                                                                      scale=scale_col[:, kc:kc + 1],
                                                                      bias=shift_col[:, kc:kc + 1]), dst, src, kc),
                         reads=(bank_t[b], T_vec), writes=(T_hT,), merge=mg)
                else:
                    P.op("dve", mk(lambda e, d_, s_, kc: e.tensor_scalar(d_, s_, scale_col[:, kc:kc + 1],
                                                                         shift_col[:, kc:kc + 1], ALU.mult, ALU.add),
                                   dst, src, kc),
                         reads=(bank_t[b], T_vec), writes=(T_hT,), merge=mg)

    import os
    KP1 = int(os.environ.get('KP1', '9'))

    def phase_P1(l):
        SB.reset(PERSIST)
        hT = SB.bf16(KC * TB)
        hT3 = hT.rearrange("p (k t) -> p k t", k=KC)
        xt = [SB.f32(D) for _ in range(2)]
        pet = [SB.f32(D) for _ in range(2)]
        xn = [SB.bf16(D) for _ in range(2)]
        junk = SB.bf16(D)
        ss = [SB.f32(4) for _ in range(2)]
        wt = [SB.bf16(KC * 512) for _ in range(2)]
        so = [SB.f32(TB) for _ in range(3)]
        T_hT = P.tile("hT")
        T_xt = [P.tile("xt%d" % i) for i in range(2)]
        T_pet = [P.tile("pet%d" % i) for i in range(2)]
        T_xn = [P.tile("xn%d" % i) for i in range(2)]
        T_junk = P.tile("junk")
        T_ss = [P.tile("ss%d" % i) for i in range(2)]
        T_wt = [P.tile("wt%d" % i) for i in range(2)]
        T_so = [P.tile("so%d" % i) for i in range(3)]
        rot = Rot([2, 3, 4, 5, 6, 7])
        soi = [0]
        NCG = N_IN // 512
        cgen = None
        if BGCONV:
            bg_items = [(l, n_) for n_ in REST] + ([(l + 1, "in")] if l + 1 < DEPTH else [])
            stg_f1 = [SB.f32(2048) for _ in range(NST)]
            stg_b1 = [SB.bf16(2048) for _ in range(NST)]
            cgen = conv_gen(bg_items, stg_f1, stg_b1, [P.tile("sf1_%d" % i) for i in range(NST)],
                            [P.tile("sb1_%d" % i) for i in range(NST)], "sp", "pool", ["dve", "act"])

        def wload(cg):
            i = cg % 2
            P.dma("sp", mk(lambda e, i, cg: e.dma_start(
                out=wt[i].rearrange("p (k n) -> p k n", n=512),
                in_=wb_in[l, :, cg * 512:(cg + 1) * 512].rearrange("(k p) n -> p k n", p=128)), i, cg),
                T_wt[i], reads=(T_w[("in", l)],), writes=(T_wt[i],))

        for tb in range(NB):
            tok0 = tb * TB
            wload(0)
            for tt in range(TBT):
                i = tt % 2
                g = tb * TBT + tt
                r0 = g * 128
                src = xin if l == 0 else xs
                P.dma("sp", mk(lambda e, i, r0, src: e.dma_start(out=xt[i], in_=src[r0:r0 + 128, :]), i, r0, src),
                      T_xt[i], reads=(() if l == 0 else (T_x[g],)), writes=(T_xt[i],))
                if l == 0:
                    P.dma("sp", mk(lambda e, i, r0: e.dma_start(out=pet[i], in_=pe_in[r0:r0 + 128, :]), i, r0),
                          T_pet[i], writes=(T_pet[i],))
                    P.op("pool", mk(lambda e, i: e.tensor_add(xt[i], xt[i], pet[i]), i),
                         reads=(T_xt[i], T_pet[i]), writes=(T_xt[i],))
                    P.dma("pool", mk(lambda e, i, r0: e.dma_start(out=xs[r0:r0 + 128, :], in_=xt[i]), i, r0),
                          T_xt[i], reads=(T_xt[i],), writes=(T_x[g],))
                if KP1 >= 1:
                    norm_T(xt[i], T_xt[i], xn[i], T_xn[i], junk, T_junk, ss[i], T_ss[i], hT3, T_hT, tt * 128,
                           v_sc1[l], v_sh1[l], first=(tt == 0))
            for cg in range(NCG if KP1 >= 2 else 0):
                if cg + 1 < NCG:
                    wload(cg + 1)
                w3 = wt[cg % 2].rearrange("p (k n) -> p k n", n=512)
                Tw = T_wt[cg % 2]
                if cg in (6, 7):
                    for tt in range(TBT):
                        b = rot.next()
                        for k in range(KC):
                            P.op("pe", mk(lambda e, b, k, tt, w3: e.matmul(banks[b][:], hT3[:, k, tt * 128:(tt + 1) * 128],
                                                                          w3[:, k, :], start=(k == 0), stop=(k == KC - 1)),
                                          b, k, tt, w3),
                                 reads=(T_hT, Tw), writes=(bank_t[b],), merge=(k > 0))
                        j = soi[0] % 3
                        soi[0] += 1
                        sb_ = so[j].bitcast(BF16)[:, 0:512]
                        P.op("dve" if tt % 2 else "act",
                             mk((lambda e, sb_, b: e.tensor_copy(sb_, banks[b][:])) if tt % 2 else
                                (lambda e, sb_, b: e.copy(sb_, banks[b][:])), sb_, b),
                             reads=(bank_t[b],), writes=(T_so[j],))
                        r0 = tok0 + tt * 128
                        c0 = (cg - 6) * 512
                        P.dma("pool", mk(lambda e, sb_, r0, c0: e.dma_start(out=vv[r0:r0 + 128, c0:c0 + 512], in_=sb_),
                                         sb_, r0, c0),
                              T_so[j], reads=(T_so[j],), writes=(dtl("vv", r0, c0),))
                    continue
                for cc in range(4):
                    j = soi[0] % 3
                    soi[0] += 1
                    isf32 = cg < 6
                    stg = so[j] if isf32 else so[j].bitcast(BF16)[:, 0:TB]
                    for hf in range(NH5):
                        b = rot.next()
                        for k in range(KC):
                            P.op("pe", mk(lambda e, b, k, cc, hf, w3: e.matmul(
                                banks[b][:], w3[:, k, cc * 128:(cc + 1) * 128], hT3[:, k, hf * 512:(hf + 1) * 512],
                                start=(k == 0), stop=(k == KC - 1)), b, k, cc, hf, w3),
                                reads=(T_hT, Tw), writes=(bank_t[b],), merge=(k > 0))
                        dst = stg[:, hf * 512:(hf + 1) * 512]
                        if cg in (8, 9):
                            P.op("act", mk(lambda e, d_, b: e.activation(out=d_, in_=banks[b][:], func=AF.Silu), dst, b),
                                 reads=(bank_t[b],), writes=(T_so[j],), merge=(hf > 0))
                        elif cg >= 12:
                            P.op("act", mk(lambda e, d_, b: e.activation(out=d_, in_=banks[b][:], func=AF.Sigmoid), dst, b),
                                 reads=(bank_t[b],), writes=(T_so[j],), merge=(hf > 0))
                        elif (cc + hf) % 2 == 0:
                            P.op("dve", mk(lambda e, d_, b: e.tensor_copy(d_, banks[b][:]), dst, b),
                                 reads=(bank_t[b],), writes=(T_so[j],), merge=(hf > 0))
                        else:
                            P.op("act", mk(lambda e, d_, b: e.copy(d_, banks[b][:]), dst, b),
                                 reads=(bank_t[b],), writes=(T_so[j],), merge=(hf > 0))
                    if isf32:
                        fc = cg * 4 + cc
                        P.dma("pool", mk(lambda e, stg, fc, tok0: e.dma_start(out=pq[fc, :, tok0:tok0 + TB], in_=stg),
                                         stg, fc, tok0),
                              T_so[j], reads=(T_so[j],), writes=(dtl("pq", fc, tb),))
                    else:
                        fc = (cg - 8) * 4 + cc
                        P.dma("pool", mk(lambda e, stg, fc, tok0: e.dma_start(out=pb[fc, :, tok0:tok0 + TB], in_=stg),
                                         stg, fc, tok0),
                              T_so[j], reads=(T_so[j],), writes=(dtl("pb", fc, tb),))
                if cgen is not None:
                    for _ in range(4):
                        next(cgen, None)
        if cgen is not None:
            for _ in cgen:
                pass
        P.barrier()

    KP2 = int(os.environ.get('KP2', '9'))

    def phase_P2a(l):
        SB.reset(PERSIST)
        NH = min(512, NT)
        NHH = NT // NH
        qf = SB.f32(NT)
        gin = [SB.f32(NT) for _ in range(2)]
        vtok = SB.bf16(NT)
        vtok3 = vtok.rearrange("p (t c) -> p t c", c=128)
        qt = [SB.bf16(NT) for _ in range(2)]
        kt = [SB.bf16(NT) for _ in range(2)]
        khT = [SB.bf16(NT) for _ in range(2)]
        sog = khT[0]
        vm = [SB.bf16(NT) for _ in range(4)]
        vm3 = [a_.rearrange("p (t c) -> p t c", c=128) for a_ in vm]
        kh = [SB.bf16(NT) for _ in range(2)]
        kh3 = [a.rearrange("p (t c) -> p t c", c=128) for a in kh]
        el = [SB.f32(NCHK) for _ in range(2)]
        em = [SB.f32(NCHK) for _ in range(2)]
        wksets = [[SB.f32(NH) for _ in range(4)] for _ in range(2)]
        cmk = SB.f32(NH)
        S = [SB.f32(128) for _ in range(2)]
        NR = 8
        Sb = [[SB.bf16(128) for _ in range(NR)] for _ in range(2)]
        AT = [[SB.bf16(128) for _ in range(2)] for _ in range(2)]
        sout = [SB.f32(128) for _ in range(4)]
        ostg = [SB.bf16(512) for _ in range(2)]
        T_q = P.tile("q")
        T_g = [P.tile("g%d" % d) for d in range(2)]
        T_v = P.tile("v")
        T_vm = P.tile("vm")
        T_qt = [P.tile("qt%d" % d) for d in range(2)]
        T_kt = [P.tile("kt%d" % d) for d in range(2)]
        T_khT = [P.tile("khT%d" % d) for d in range(2)]
        T_sog = T_khT[0]
        T_kh = [P.tile("kh%d" % d) for d in range(2)]
        T_el = [P.tile("el%d" % d) for d in range(2)]
        T_wksets = [[P.tile("wk%d_%d" % (s_, i)) for i in range(4)] for s_ in range(2)]
        T_cmk = P.tile("cmk")
        T_S = [P.tile("S%d" % d) for d in range(2)]
        T_Sb = [[P.tile("Sb%d_%d" % (d, i)) for i in range(NR)] for d in range(2)]
        T_AT = [[P.tile("AT%d_%d" % (d, i)) for i in range(2)] for d in range(2)]
        T_sout = [P.tile("sout%d" % i) for i in range(4)]
        T_ostg = [P.tile("ostg%d" % i) for i in range(2)]
        souti = [0]
        ostgi = [0]
        pit = [0]

        P.op("dve", lambda e: e.memset(cmk, 1.0), writes=(T_cmk,))
        for d in range(2):
            for par in range(2):
                P.op("dve", mk(lambda e, d, par: e.memset(AT[d][par], 0.0), d, par), writes=(T_AT[d][par],))

        for h in range(H_A):
            P.dma("sp", mk(lambda e, h: e.dma_start(out=qf, in_=pq[h]), h), T_q, writes=(T_q,))
            for d in range(2):
                P.dma("sp", mk(lambda e, h, d: e.dma_start(out=gin[d], in_=pq[8 + 8 * d + h]), h, d), T_g[d], writes=(T_g[d],))
            P.dma("sp", mk(lambda e, h: e.dma_start(out=vtok3, in_=vv[:, h * 128:(h + 1) * 128].rearrange("(t p) c -> p t c", p=128)), h),
                  T_v, writes=(T_v,))
            for c in range(4):
                if c % 2 == 0:
                    P.op("dve", mk(lambda e, c: e.tensor_scalar_mul(vm[c], vtok, c_rm[:, c:c + 1]), c), reads=(T_v, T_const),
                         writes=(T_vm,), merge=(c > 0))
                else:
                    P.op("act", mk(lambda e, c: e.activation(out=vm[c], in_=vtok, func=AF.Copy, scale=c_rm[:, c:c + 1]), c),
                         reads=(T_v, T_const), writes=(T_vm,), merge=(c > 0))
            for d in range(2):
                P.dma("sp", mk(lambda e, h, d: e.dma_start(out=S[d], in_=s0[l, d, h]), h, d), T_S[d], writes=(T_S[d],))
            for d in range(2 if KP2 >= 1 else 0):
                if l == 0:
                    lbv, omlv, nomlv = 0.0, 1.0, -1.0
                else:
                    lbv = v_lb[:, d * 8 + h:d * 8 + h + 1]
                    omlv = v_oml[:, d * 8 + h:d * 8 + h + 1]
                def prep_iter(d, hh, wk, T_wk, lbv, omlv):
                    c0 = hh * NH
                    A, B, C, Dd = wk
                    TA, TB_, TC, TD = T_wk
                    gsl = gin[d][:, c0:c0 + NH]
                    P.op("act", mk(lambda e, gsl: e.activation(out=A, in_=gsl, func=AF.Sigmoid), gsl),
                         reads=(T_g[d],), writes=(TA,))
                    if l == 0:
                        P.op("dve", lambda e: e.tensor_scalar(C, A, -1.0, 1.0, ALU.mult, ALU.add), reads=(TA,), writes=(TC,))
                        P.op("act", lambda e: e.activation(out=Dd, in_=A, func=AF.Ln), reads=(TA,), writes=(TD,))
                    else:
                        P.op("dve", mk(lambda e, omlv, lbv: e.tensor_scalar(B, A, omlv, lbv, ALU.mult, ALU.add), omlv, lbv),
                             reads=(TA, T_vec), writes=(TB_,))
                        P.op("dve", mk(lambda e, omlv: e.tensor_scalar(C, A, -1.0, omlv, ALU.add, ALU.mult), omlv),
                             reads=(TA, T_vec), writes=(TC,))
                        P.op("dve", lambda e: e.tensor_scalar_mul(C, C, -1.0), reads=(TC,), writes=(TC,))
                        P.op("act", lambda e: e.activation(out=Dd, in_=B, func=AF.Ln), reads=(TB_,), writes=(TD,))
                    P.op("dve", lambda e: e.tensor_tensor_scan(B, cmk, Dd, 0.0, ALU.mult, ALU.add),
                         reads=(T_cmk, TD), writes=(TB_,))
                    B3 = B.rearrange("p (c t) -> p c t", t=CH)
                    A3 = A.rearrange("p (c t) -> p c t", t=CH)
                    D3 = Dd.rearrange("p (c t) -> p c t", t=CH)
                    nch = NH // CH
                    if d == 0:
                        P.op("dve", lambda e: e.tensor_tensor(A3, B3[:, :, 0:1].to_broadcast([128, nch, CH]), B3, ALU.subtract),
                             reads=(TB_,), writes=(TA,))
                        P.op("dve", lambda e: e.tensor_tensor(A3, D3[:, :, 0:1].to_broadcast([128, nch, CH]), A3, ALU.subtract),
                             reads=(TA, TD), writes=(TA,))
                        X, TX, X3, Y, TY, Y3 = A, TA, A3, B, TB_, B3
                        lastc = CH - 1
                    else:
                        P.op("dve", lambda e: e.tensor_tensor(A3, B3[:, :, CH - 1:CH].to_broadcast([128, nch, CH]), B3, ALU.subtract),
                             reads=(TB_,), writes=(TA,))
                        P.op("dve", lambda e: e.tensor_add(A, A, Dd), reads=(TA, TD), writes=(TA,))
                        X, TX, X3, Y, TY, Y3 = A, TA, A3, B, TB_, B3
                        lastc = 0
                    MID = CH // 2
                    P.op("act", mk(lambda e, X3, d, hh, lc: e.activation(out=el[d][:, hh * nch:(hh + 1) * nch].unsqueeze(2),
                                                                        in_=X3[:, :, lc:lc + 1], func=AF.Exp), X3, d, hh, lastc),
                         reads=(TX,), writes=(T_el[d],), merge=(hh > 0))
                    P.op("act", mk(lambda e, X3, d, hh: e.activation(out=em[d][:, hh * nch:(hh + 1) * nch].unsqueeze(2),
                                                                    in_=X3[:, :, MID:MID + 1], func=AF.Exp), X3, d, hh),
                         reads=(TX,), writes=(T_el[d],), merge=True)
                    P.op("dve", mk(lambda e, X3, lc: e.tensor_tensor(D3, X3[:, :, lc:lc + 1].to_broadcast([128, nch, CH]), X3,
                                                                     ALU.subtract), X3, lastc),
                         reads=(TX,), writes=(TD,))
                    P.op("act", lambda e: e.activation(out=Dd, in_=Dd, func=AF.Exp), reads=(TD,), writes=(TD,))
                    P.op("pool", mk(lambda e, d, c0: e.tensor_mul(khT[d][:, c0:c0 + NH], C, Dd), d, c0),
                         reads=(TC, TD), writes=(T_khT[d],), merge=(hh > 0))
                    P.op("dve", mk(lambda e, X3: e.tensor_tensor(D3, X3[:, :, MID:MID + 1].to_broadcast([128, nch, CH]), X3,
                                                                 ALU.subtract), X3),
                         reads=(TX,), writes=(TD,))
                    P.op("act", mk(lambda e, Y: e.activation(out=Y, in_=Dd, func=AF.Exp), Y), reads=(TD,), writes=(TY,))
                    P.op("pool", mk(lambda e, Y, d, c0: e.tensor_mul(kt[d][:, c0:c0 + NH], C, Y), Y, d, c0),
                         reads=(TC, TY), writes=(T_kt[d],), merge=(hh > 0))
                    P.op("act", mk(lambda e, Y: e.activation(out=Y, in_=Dd, func=AF.Exp, scale=-1.0), Y), reads=(TD,), writes=(TY,))
                    P.op("pool", mk(lambda e, Y, d, c0: e.tensor_mul(qt[d][:, c0:c0 + NH], qf[:, c0:c0 + NH], Y), Y, d, c0),
                         reads=(T_q, TY), writes=(T_qt[d],), merge=(hh > 0))
                for hh in range(0, NHH, 2):
                    caps = []
                    for q_ in range(min(2, NHH - hh)):
                        P.capture()
                        prep_iter(d, hh + q_, wksets[q_], T_wksets[q_], lbv, omlv)
                        caps.append(P.end_capture())
                    P.replay(caps)
                for g8 in range(0, NTT if KP2 >= 2 else 0, 8):
                    b = (g8 // 8) % 2
                    n8 = min(8, NTT - g8)
                    for j in range(n8):
                        P.op("pe", mk(lambda e, b, j, g8, d: e.transpose(bankbf[b][:, j * 128:(j + 1) * 128],
                                                                          khT[d][:, (g8 + j) * 128:(g8 + j + 1) * 128], c_identb),
                                      b, j, g8, d),
                             reads=(T_khT[d], T_const), writes=(bank_t[b],), merge=(j > 0))
                    P.op("act" if b else "dve",
                         mk((lambda e, b, g8, n8, d: e.copy(kh[d][:, g8 * 128:(g8 + n8) * 128], bankbf[b][:, 0:n8 * 128])) if b else
                            (lambda e, b, g8, n8, d: e.tensor_copy(kh[d][:, g8 * 128:(g8 + n8) * 128], bankbf[b][:, 0:n8 * 128])),
                            b, g8, n8, d),
                         reads=(bank_t[b],), writes=(T_kh[d],), merge=(g8 > 0))
            if debug and l == 0 and h == 0:
                for d in range(2):
                    for nm, src_, tl_ in (("qt", qt[d], T_qt[d]), ("kt", kt[d], T_kt[d]), ("khT", khT[d], T_khT[d]), ("kh", kh[d], T_kh[d])):
                        dd = nc.dram_tensor("dbg_%s%d" % (nm, d), [128, NT], BF16, kind="ExternalOutput").ap()
                        P.dma("sp", mk(lambda e, dd, src_: e.dma_start(out=dd, in_=src_), dd, src_), tl_, reads=(tl_,))
                    dd = nc.dram_tensor("dbg_el%d" % d, [128, NCHK], F32, kind="ExternalOutput").ap()
                    P.dma("sp", mk(lambda e, dd, d: e.dma_start(out=dd, in_=el[d]), dd, d), T_el[d], reads=(T_el[d],))
            P.dma("sp", mk(lambda e, h: e.dma_start(out=sog, in_=pb[h]), h), T_sog, writes=(T_sog,))
            ring = [0, 0]
            for d in range(2):
                g0 = 0 if d == 0 else NCHK - 1
                P.op("act", mk(lambda e, d, g0: e.activation(out=Sb[d][0], in_=S[d], func=AF.Copy, scale=em[d][:, g0:g0 + 1]), d, g0),
                     reads=(T_S[d], T_el[d]), writes=(T_Sb[d][0],))
            for i in range(NTT if KP2 >= 3 else 0):
                tiles = (i, NTT - 1 - i)
                par = i % 2
                for d in range(2):
                    j = tiles[d]
                    bA = d
                    P.op("pe", mk(lambda e, bA, d, j: e.matmul(banks[bA][:, 0:128], kt[d][:, j * 128:(j + 1) * 128],
                                                               qt[d][:, j * 128:(j + 1) * 128], start=True, stop=True), bA, d, j),
                         reads=(T_kt[d], T_qt[d]), writes=(bank_t[bA],))
                    P.op("dve", mk(lambda e, bA, d, par: e.copy_predicated(AT[d][par], (c_mf if d == 0 else c_mb).bitcast(mybir.dt.uint32),
                                                                            banks[bA][:, 0:128]), bA, d, par),
                         reads=(bank_t[bA], T_const), writes=(T_AT[d][par],))
                    bU = 2 + d * 2 + par
                    for c in range(4):
                        P.op("pe", mk(lambda e, bU, c, d, j: e.matmul(banks[bU][:, c * 128:(c + 1) * 128],
                                                                      kh3[d][:, j, :], vm3[c][:, j, :],
                                                                      start=True, stop=True), bU, c, d, j),
                             reads=(T_kh[d], T_vm), writes=(bank_t[bU],), merge=(c > 0))
                par = i % 2
                for d in range(2):
                    j = tiles[d]
                    bO = 6 + d
                    P.op("pe", mk(lambda e, bO, d, j, par: e.matmul(banks[bO][:, 0:128], vtok3[:, j, :], AT[d][par],
                                                                    start=True, stop=False), bO, d, j, par),
                         reads=(T_v, T_AT[d][par]), writes=(bank_t[bO],))
                for n_ in range(4):
                    for d in range(2):
                        j = tiles[d]
                        bU = 2 + d * 2 + par
                        bO = 6 + d
                        c = n_ if d == 0 else 3 - n_
                        gc = j * 4 + c
                        r = ring[d]
                        col = j * 128 + c * 32
                        P.op("pe", mk(lambda e, bO, c, d, r, col, n_: e.matmul(banks[bO][:, c * 32:(c + 1) * 32], Sb[d][r],
                                                                               qt[d][:, col:col + 32], start=False, stop=(n_ == 3)),
                                      bO, c, d, r, col, n_),
                             reads=(T_Sb[d][r], T_qt[d]), writes=(bank_t[bO],), merge=True)
                        P.op("dve", mk(lambda e, d, gc, bU, c: e.scalar_tensor_tensor(S[d], S[d], el[d][:, gc:gc + 1],
                                                                                      banks[bU][:, c * 128:(c + 1) * 128],
                                                                                      ALU.mult, ALU.add), d, gc, bU, c),
                             reads=(T_S[d], T_el[d], bank_t[bU]), writes=(T_S[d],))
                        seg_end = (gc % 8 == 7) if d == 0 else (gc % 8 == 0)
                        if seg_end:
                            seg = gc // 8
                            so_i = souti[0] % 4
                            souti[0] += 1
                            P.op("act", mk(lambda e, so_i, d: e.copy(sout[so_i], S[d]), so_i, d), reads=(T_S[d],),
                                 writes=(T_sout[so_i],))
                            P.dma("pool", mk(lambda e, so_i, seg, d, h: e.dma_start(out=st_out[l, seg, d, h], in_=sout[so_i]),
                                             so_i, seg, d, h),
                                  T_sout[so_i], reads=(T_sout[so_i],))
                            last_seg = (seg == NSEG - 1) if d == 0 else (seg == 0)
                            if not last_seg:
                                P.op("dve", mk(lambda e, d: e.tensor_scalar_mul(S[d], S[d], c_carry[:, 0:1]), d),
                                     reads=(T_S[d], T_const), writes=(T_S[d],))
                        ring[d] = (r + 1) % NR
                        r2 = ring[d]
                        gn = gc + 1 if d == 0 else gc - 1
                        if 0 <= gn < NCHK:
                            P.op("act", mk(lambda e, d, r2, gn: e.activation(out=Sb[d][r2], in_=S[d], func=AF.Copy,
                                                                             scale=em[d][:, gn:gn + 1]), d, r2, gn),
                                 reads=(T_S[d], T_el[d]), writes=(T_Sb[d][r2],))
                for d in range(2):
                    j = tiles[d]
                    bO = 6 + d
                    P.op("act", mk(lambda e, d, j, bO: e.copy(gin[d][:, j * 128:(j + 1) * 128], banks[bO][:, 0:128]), d, j, bO),
                         reads=(bank_t[bO],), writes=(T_g[d],), merge=True)
            if debug and l == 0 and h == 0:
                for d in range(2):
                    dd = nc.dram_tensor("dbg_o%d" % d, [128, NT], F32, kind="ExternalOutput").ap()
                    P.dma("sp", mk(lambda e, dd, d: e.dma_start(out=dd, in_=gin[d]), dd, d), T_g[d], reads=(T_g[d],))
            def hn_block(blk, bufs, tls, h):
                c0 = blk * 512
                o_, sq, rs = bufs[0], bufs[1], bufs[2]
                To, Tq_, Tr = tls[0], tls[1], tls[2]
                P.op("dve", lambda e: e.tensor_add(o_, gin[0][:, c0:c0 + 512], gin[1][:, c0:c0 + 512]),
                     reads=(T_g[0], T_g[1]), writes=(To,))
                P.op("act", lambda e: e.activation(out=sq, in_=o_, func=AF.Square), reads=(To,), writes=(Tq_,))
                bN = blk % 2
                P.op("pe", lambda e: e.matmul(banks[bN][:], c_ones, sq, start=True, stop=True),
                     reads=(T_const, Tq_), writes=(bank_t[bN],))
                P.op("act", lambda e: e.activation(out=rs, in_=banks[bN][:], func=AF.Sqrt, scale=1.0 / 128, bias=c_eps),
                     reads=(bank_t[bN], T_const), writes=(Tr,))
                P.op("dve", lambda e: e.reciprocal(rs, rs), reads=(Tr,), writes=(Tr,))
                P.op("dve", lambda e: e.tensor_mul(o_, o_, rs), reads=(To, Tr), writes=(To,))
                oi = ostgi[0] % 2
                ostgi[0] += 1
                P.op("dve", lambda e: e.scalar_tensor_tensor(ostg[oi], o_, v_gn[:, l:l + 1], sog[:, c0:c0 + 512], ALU.mult, ALU.mult),
                     reads=(To, T_vec, T_sog), writes=(T_ostg[oi],))
                P.dma("pool", lambda e: e.dma_start(out=og_s[h, :, c0:c0 + 512], in_=ostg[oi]), T_ostg[oi], reads=(T_ostg[oi],))

            for blk in range(0, NT // 512 if KP2 >= 4 else 0, 2):
                caps = []
                for q_ in range(min(2, NT // 512 - blk)):
                    P.capture()
                    hn_block(blk + q_, wksets[q_], T_wksets[q_], h)
                    caps.append(P.end_capture())
                P.replay(caps)
        P.barrier()

    def phase_P2b(l):
        SB.reset(PERSIST)
        NG = 2 if NT > 2048 else 4
        cf = SB.bf16(2 * 512)
        cf3 = cf.rearrange("p (k n) -> p k n", n=512)
        Aall = SB.bf16(NTT * NG * 512)
        A4 = Aall.rearrange("p (t g n) -> p t g n", g=NG, n=512)
        ut = [SB.bf16(2 * 512) for _ in range(2)]
        NDT = 4
        dt_ = [SB.bf16(NDT * 2 * 512) for _ in range(3)]
        zst = [SB.bf16(512) for _ in range(4)]
        T_cf = P.tile("cf")
        T_A = P.tile("Aall")
        T_ut = [P.tile("ut%d" % i) for i in range(2)]
        T_dt = [P.tile("dt%d" % i) for i in range(3)]
        T_zst = [P.tile("zst%d" % i) for i in range(4)]
        P.dma("sp", lambda e: e.dma_start(out=cf3, in_=cfsf.rearrange("(k p) n -> p k n", p=128)), T_cf, writes=(T_cf,))
        uti = [0]
        dti = [0]
        zi = [0]
        rot = Rot(range(8))
        for gp in range(4 // NG):
            first = True
            for g in range(NG):
                gg = gp * NG + g
                for t5 in range(NT // 512):
                    i = uti[0] % 2
                    uti[0] += 1
                    u3 = ut[i].rearrange("p (k n) -> p k n", n=512)
                    P.dma("sp", mk(lambda e, u3, gg, t5: e.dma_start(out=u3, in_=pb[8 + 2 * gg:8 + 2 * gg + 2, :, t5 * 512:(t5 + 1) * 512]
                                                                    .rearrange("k p n -> p k n")), u3, gg, t5),
                          T_ut[i], writes=(T_ut[i],))
                    for t1 in range(4):
                        tt = t5 * 4 + t1
                        b = rot.next()
                        for k in range(2):
                            P.op("pe", mk(lambda e, b, u3, k, t1: e.matmul(banks[b][:], u3[:, k, t1 * 128:(t1 + 1) * 128], cf3[:, k, :],
                                                                          start=(k == 0), stop=(k == 1)), b, u3, k, t1),
                                 reads=(T_ut[i], T_cf), writes=(bank_t[b],), merge=(k > 0))
                        P.op("act" if tt % 2 else "dve",
                             mk((lambda e, b, tt, g: e.copy(A4[:, tt, g, :], banks[b][:])) if tt % 2 else
                                (lambda e, b, tt, g: e.tensor_copy(A4[:, tt, g, :], banks[b][:])), b, tt, g),
                             reads=(bank_t[b],), writes=(T_A,), merge=(not first))
                        first = False
            for kb in range(NT // 512):
                nacc = 2 * NG
                bs = [(kb % 2) * 4 + a for a in range(nacc)] if nacc <= 4 else list(range(8))
                for n0 in range(0, NTT, NDT):
                    nn = min(NDT, NTT - n0)
                    i = dti[0] % 3
                    dti[0] += 1
                    d4 = dt_[i].rearrange("p (t c n) -> p t c n", c=2, n=512)
                    for cs in range(2):
                        P.dma("sp", mk(lambda e, d4, cs, n0, nn, kb: e.dma_start(
                            out=d4[:, 0:nn, cs, :],
                            in_=dn[cs, n0 * 128:(n0 + nn) * 128, kb * 512:(kb + 1) * 512].rearrange("(t p) n -> p t n", p=128)),
                            d4, cs, n0, nn, kb), T_dt[i], writes=(T_dt[i],), merge=(cs > 0))
                    for t in range(nn):
                        nt_ = n0 + t
                        for cs in range(2):
                            for g in range(NG):
                                for mc in range(2):
                                    b = bs[g * 2 + mc]
                                    fst = (nt_ == 0 and cs == 0)
                                    lst = (nt_ == NTT - 1 and cs == 1)
                                    P.op("pe", mk(lambda e, b, nt_, g, cs, mc, d4, t, fst, lst: e.matmul(
                                        banks[b][:], A4[:, nt_, g, cs * 256 + mc * 128:cs * 256 + (mc + 1) * 128], d4[:, t, cs, :],
                                        start=fst, stop=lst), b, nt_, g, cs, mc, d4, t, fst, lst),
                                        reads=(T_A, T_dt[i]), writes=(bank_t[b],), merge=(not fst))
                for g in range(NG):
                    for mc in range(2):
                        b = bs[g * 2 + mc]
                        j = zi[0] % 4
                        zi[0] += 1
                        P.op("act" if mc else "dve",
                             mk((lambda e, j, b: e.copy(zst[j], banks[b][:])) if mc else
                                (lambda e, j, b: e.tensor_copy(zst[j], banks[b][:])), j, b),
                             reads=(bank_t[b],), writes=(T_zst[j],))
                        zc = (gp * NG + g) * 2 + mc
                        P.dma("pool", mk(lambda e, j, zc, kb: e.dma_start(out=z_s[zc, :, kb * 512:(kb + 1) * 512], in_=zst[j]), j, zc, kb),
                              T_zst[j], reads=(T_zst[j],))
        P.barrier()

    def phase_P3(l):
        SB.reset(PERSIST)
        TB3 = 512
        NB3 = NT // TB3
        ogT = SB.bf16(8 * TB3)
        zT = SB.bf16(8 * TB3)
        og3 = ogT.rearrange("p (k t) -> p k t", k=8)
        z3 = zT.rearrange("p (k t) -> p k t", k=8)
        mh = SB.bf16(KC * TB3)
        mh3 = mh.rearrange("p (k t) -> p k t", k=KC)
        actT = SB.bf16(FC * TB3)
        act3 = actT.rearrange("p (k t) -> p k t", k=FC)
        wt_all = SB.bf16(2 * KC * 512)
        wh = [wt_all[:, i * 4096:(i + 1) * 4096] for i in range(4)]
        sg = [SB.bf16(4 * TB3) for _ in range(2)]
        gb1 = SB.f32(D)
        gb2 = SB.f32(D)
        nfb = SB.f32(D)
        xt = [SB.f32(D) for _ in range(4)]
        xn = [SB.bf16(D)] * 2
        junk = SB.bf16(D)
        ss = [SB.f32(4) for _ in range(2)]
        tmp = [SB.f32(512) for _ in range(3)]
        T_og = P.tile("ogT")
        T_z = P.tile("zT")
        T_mh = P.tile("mh")
        T_act = P.tile("actT")
        T_wh = [P.tile("wh%d" % i) for i in range(4)]
        T_sg = [P.tile("sg%d" % i) for i in range(2)]
        T_gb = P.tile("gb")
        T_xt = [P.tile("xt%d" % i) for i in range(4)]
        T_xn = [P.tile("xn")] * 2
        T_junk = P.tile("junk")
        T_ss = [P.tile("ss%d" % i) for i in range(2)]
        T_tmp = [P.tile("tmp%d" % i) for i in range(3)]
        rot = Rot([2, 3, 4, 5, 6, 7])
        wi = [0]
        sgi = [0]
        tmi = [0]
        last = (l == DEPTH - 1)
        P.dma("sp", lambda e: e.dma_start(out=gb1, in_=g_s[l, 0].partition_broadcast(128)), T_gb, writes=(T_gb,))
        P.dma("sp", lambda e: e.dma_start(out=gb2, in_=g_s[l, 1].partition_broadcast(128)), T_gb, writes=(T_gb,), merge=True)
        if last:
            P.dma("sp", lambda e: e.dma_start(out=nfb, in_=norm_final.partition_broadcast(128)), T_gb, writes=(T_gb,), merge=True)

        def wload(src, r0, nk, c0, key, ncol=512):
            if nk * ncol <= 4096:
                i = wi[0] % 4
                wi[0] += 1
                buf, tls = wh[i], (T_wh[i],)
            else:
                if wi[0] % 2:
                    wi[0] += 1
                i = wi[0] % 4
                wi[0] += 2
                buf, tls = wt_all[:, i * 4096:(i + 2) * 4096], (T_wh[i], T_wh[i + 1])
            w3 = buf[:, 0:nk * ncol].rearrange("p (k n) -> p k n", n=ncol)
            P.dma("sp", lambda e: e.dma_start(out=w3, in_=src[r0:r0 + nk * 128, c0:c0 + ncol].rearrange("(k p) n -> p k n", p=128)),
                  tls[0], reads=(T_w[key],), writes=tls)
            return w3, tls

        for tb in range(NB3):
            tok0 = tb * TB3
            P.dma("sp", mk(lambda e, tok0: e.dma_start(out=og3, in_=og_s[:, :, tok0:tok0 + TB3].rearrange("k p n -> p k n")), tok0),
                  T_og, writes=(T_og,))
            P.dma("sp", mk(lambda e, tok0: e.dma_start(out=z3, in_=z_s[:, :, tok0:tok0 + TB3].rearrange("k p n -> p k n")), tok0),
                  T_z, writes=(T_z,))
            for tt in range(4):
                r0 = tok0 + tt * 128
                P.dma("sp", mk(lambda e, tt, r0: e.dma_start(out=xt[tt], in_=xs[r0:r0 + 128, :]), tt, r0),
                      T_xt[tt], reads=(T_x[r0 // 128],), writes=(T_xt[tt],))
            for dg in range(4):
                wa3, Twa = wload(wb_a[l], 0, 8, dg * 512, ("a", l))
                wb3, Twb = wload(wb_b[l], 0, 8, dg * 512, ("b", l))
                sga_i = sgi[0] % 2
                sgi[0] += 1
                sga = sg[sga_i].rearrange("p (k t) -> p k t", k=4)
                P.dma("sp", mk(lambda e, sga, dg, tok0: e.dma_start(
                    out=sga, in_=pb[16 + dg * 4:16 + dg * 4 + 4, :, tok0:tok0 + TB3].rearrange("k p n -> p k n")), sga, dg, tok0),
                    T_sg[sga_i], writes=(T_sg[sga_i],))
                sgb_i = sgi[0] % 2
                sgi[0] += 1
                sgb = sg[sgb_i].rearrange("p (k t) -> p k t", k=4)
                P.dma("sp", mk(lambda e, sgb, dg, tok0: e.dma_start(
                    out=sgb, in_=pb[32 + dg * 4:32 + dg * 4 + 4, :, tok0:tok0 + TB3].rearrange("k p n -> p k n")), sgb, dg, tok0),
                    T_sg[sgb_i], writes=(T_sg[sgb_i],))
                for cc in range(4):
                    dc = dg * 4 + cc
                    ba = rot.next()
                    bb = rot.next()
                    for k in range(8):
                        P.op("pe", mk(lambda e, ba, wa3, k, cc: e.matmul(banks[ba][:], wa3[:, k, cc * 128:(cc + 1) * 128], og3[:, k, :],
                                                                        start=(k == 0), stop=(k == 7)), ba, wa3, k, cc),
                             reads=Twa + (T_og,), writes=(bank_t[ba],), merge=(k > 0))
                    for k in range(8):
                        P.op("pe", mk(lambda e, bb, wb3, k, cc: e.matmul(banks[bb][:], wb3[:, k, cc * 128:(cc + 1) * 128], z3[:, k, :],
                                                                        start=(k == 0), stop=(k == 7)), bb, wb3, k, cc),
                             reads=Twb + (T_z,), writes=(bank_t[bb],), merge=(k > 0))
                    t1i = tmi[0] % 3
                    t2i = (tmi[0] + 1) % 3
                    tmi[0] += 2
                    P.op("dve", mk(lambda e, t1i, ba, sga, cc: e.tensor_mul(tmp[t1i], banks[ba][:], sga[:, cc, :]), t1i, ba, sga, cc),
                         reads=(bank_t[ba], T_sg[sga_i]), writes=(T_tmp[t1i],))
                    P.op("dve", mk(lambda e, t2i, bb, sgb, cc: e.tensor_mul(tmp[t2i], banks[bb][:], sgb[:, cc, :]), t2i, bb, sgb, cc),
                         reads=(bank_t[bb], T_sg[sgb_i]), writes=(T_tmp[t2i],))
                    P.op("pool", mk(lambda e, dc, t1i, t2i: e.tensor_add(mh3[:, dc, :], tmp[t1i], tmp[t2i]), dc, t1i, t2i),
                         reads=(T_tmp[t1i], T_tmp[t2i]), writes=(T_mh,), merge=(dc > 0))
            for db in range(4):
                w3, Tw = wload(wb_out[l], 0, KC, db * 512, ("out", l))
                for tt in range(4):
                    b = rot.next()
                    for k in range(KC):
                        P.op("pe", mk(lambda e, b, k, tt, w3: e.matmul(banks[b][:], mh3[:, k, tt * 128:(tt + 1) * 128], w3[:, k, :],
                                                                      start=(k == 0), stop=(k == KC - 1)), b, k, tt, w3),
                             reads=(T_mh,) + Tw, writes=(bank_t[b],), merge=(k > 0))
                    ti = tmi[0] % 3
                    tmi[0] += 1
                    P.op("dve", mk(lambda e, ti, b, db: e.tensor_mul(tmp[ti], banks[b][:], gb1[:, db * 512:(db + 1) * 512]), ti, b, db),
                         reads=(bank_t[b], T_gb), writes=(T_tmp[ti],))
                    P.op("pool", mk(lambda e, ti, tt, db: e.tensor_add(xt[tt][:, db * 512:(db + 1) * 512],
                                                                      xt[tt][:, db * 512:(db + 1) * 512], tmp[ti]), ti, tt, db),
                         reads=(T_tmp[ti], T_xt[tt]), writes=(T_xt[tt],))
            for tt in range(4):
                i = tt % 2
                norm_T(xt[tt], T_xt[tt], xn[i], T_xn[i], junk, T_junk, ss[i], T_ss[i], mh3, T_mh, tt * 128,
                       v_sc2[l], v_sh2[l], first=(tt == 0))
            for jg in range(22):
                wA, TwA = wload(wb_ffi[l], 0, KC, jg * 256, ("ffi", l), ncol=256)
                wG, TwG = wload(wb_ffi[l], 0, KC, D_FF + jg * 256, ("ffi", l), ncol=256)
                for cc in range(2):
                    fc = jg * 2 + cc
                    ba = rot.next()
                    bb = rot.next()
                    for k in range(KC):
                        P.op("pe", mk(lambda e, ba, wA, k, cc: e.matmul(banks[ba][:], wA[:, k, cc * 128:(cc + 1) * 128], mh3[:, k, :],
                                                                       start=(k == 0), stop=(k == KC - 1)), ba, wA, k, cc),
                             reads=TwA + (T_mh,), writes=(bank_t[ba],), merge=(k > 0))
                    for k in range(KC):
                        P.op("pe", mk(lambda e, bb, wG, k, cc: e.matmul(banks[bb][:], wG[:, k, cc * 128:(cc + 1) * 128], mh3[:, k, :],
                                                                       start=(k == 0), stop=(k == KC - 1)), bb, wG, k, cc),
                             reads=TwG + (T_mh,), writes=(bank_t[bb],), merge=(k > 0))
                    ti = tmi[0] % 3
                    tmi[0] += 1
                    P.op("act", mk(lambda e, ti, ba: e.activation(out=tmp[ti], in_=banks[ba][:], func=AF.Silu), ti, ba),
                         reads=(bank_t[ba],), writes=(T_tmp[ti],))
                    P.op("dve", mk(lambda e, ti, bb, fc: e.tensor_mul(act3[:, fc, :], tmp[ti], banks[bb][:]), ti, bb, fc),
                         reads=(T_tmp[ti], bank_t[bb]), writes=(T_act,), merge=(fc > 0))
            for db in range(4):
                bks = [rot.next() for _ in range(4)]
                for kg in range(4):
                    w3, Tw = wload(wb_ffo[l], kg * 11 * 128, 11, db * 512, ("ffo", l))
                    for tt in range(4):
                        b = bks[tt]
                        for k in range(11):
                            kk = kg * 11 + k
                            P.op("pe", mk(lambda e, b, kk, k, tt, w3: e.matmul(banks[b][:], act3[:, kk, tt * 128:(tt + 1) * 128], w3[:, k, :],
                                                                              start=(kk == 0), stop=(kk == FC - 1)), b, kk, k, tt, w3),
                                 reads=(T_act,) + Tw, writes=(bank_t[b],), merge=(kk > 0))
                for tt in range(4):
                    b = bks[tt]
                    ti = tmi[0] % 3
                    tmi[0] += 1
                    P.op("dve", mk(lambda e, ti, b, db: e.tensor_mul(tmp[ti], banks[b][:], gb2[:, db * 512:(db + 1) * 512]), ti, b, db),
                         reads=(bank_t[b], T_gb), writes=(T_tmp[ti],))
                    P.op("pool", mk(lambda e, ti, tt, db: e.tensor_add(xt[tt][:, db * 512:(db + 1) * 512],
                                                                      xt[tt][:, db * 512:(db + 1) * 512], tmp[ti]), ti, tt, db),
                         reads=(T_tmp[ti], T_xt[tt]), writes=(T_xt[tt],))
            for tt in range(4):
                r0 = tok0 + tt * 128
                if not last:
                    P.dma("pool", mk(lambda e, tt, r0: e.dma_start(out=xs[r0:r0 + 128, :], in_=xt[tt]), tt, r0),
                          T_xt[tt], reads=(T_xt[tt],), writes=(T_x[r0 // 128],))
                else:
                    i = tt % 2
                    P.op("act", mk(lambda e, tt, i: e.activation(out=junk, in_=xt[tt], func=AF.Square, accum_out=ss[i][:, 0:1]), tt, i),
                         reads=(T_xt[tt],), writes=(T_junk, T_ss[i]))
                    P.op("act", mk(lambda e, i: e.activation(out=ss[i][:, 1:2], in_=ss[i][:, 0:1], func=AF.Sqrt, scale=1.0 / D, bias=c_eps), i),
                         reads=(T_ss[i], T_const), writes=(T_ss[i],))
                    P.op("dve", mk(lambda e, i: e.reciprocal(ss[i][:, 2:3], ss[i][:, 1:2]), i), reads=(T_ss[i],), writes=(T_ss[i],))
                    P.op("dve", mk(lambda e, tt, i: e.scalar_tensor_tensor(xt[tt], xt[tt], ss[i][:, 2:3], nfb, ALU.mult, ALU.mult), tt, i),
                         reads=(T_xt[tt], T_ss[i], T_gb), writes=(T_xt[tt],))
                    P.dma("pool", mk(lambda e, tt, r0: e.dma_start(out=y_out[r0:r0 + 128, :], in_=xt[tt]), tt, r0),
                          T_xt[tt], reads=(T_xt[tt],))
        P.barrier()

    import os
    nph = int(os.environ.get("KPH", "99"))
    ph = 0
    for l in range(DEPTH):
        for f in (phase_P1, phase_P2a, phase_P2b, phase_P3):
            if ph < nph:
                f(l)
            ph += 1
    P.emit()
    return nc, stack


GRID_W = 64
POS_BASE = 10000.0


def _pos_embed(n):
    rows = n // GRID_W
    r, col = np.meshgrid(np.arange(rows, dtype=np.float32), np.arange(GRID_W, dtype=np.float32), indexing="ij")
    quarter = D // 4
    omega = (1.0 / (np.float32(POS_BASE) ** (np.arange(quarter, dtype=np.float32) / np.float32(quarter)))).astype(np.float32)

    def enc(p):
        a = (p.reshape(-1)[:, None] * omega[None, :]).astype(np.float32)
        return np.concatenate([np.sin(a), np.cos(a)], axis=-1)
    return np.concatenate([enc(r), enc(col)], axis=-1).astype(np.float32)


def _dft_tables(nt, nper):
    n = np.arange(nper)
    ang = 2.0 * np.pi * ((np.outer(n, n) % nper).astype(np.float64)) / nper
    c = np.cos(ang) / np.sqrt(nper)
    s = -np.sin(ang) / np.sqrt(nper)
    out = np.zeros((2, nt, nt), dtype=ml_dtypes.bfloat16)
    for b in range(nt // nper):
        out[0, b * nper:(b + 1) * nper, b * nper:(b + 1) * nper] = c.astype(ml_dtypes.bfloat16)
        out[1, b * nper:(b + 1) * nper, b * nper:(b + 1) * nper] = s.astype(ml_dtypes.bfloat16)
    return out


def _consts():
    n = np.arange(256)
    ang = 2.0 * np.pi * ((np.outer(n, n) % 256).astype(np.float64)) / 256
    cfsf = np.concatenate([np.cos(ang), np.sin(ang)], axis=1) / 16.0
    t = np.arange(128)
    same = (t[:, None] // CH) == (t[None, :] // CH)
    mf = (same & (t[:, None] <= t[None, :])).astype(np.float32)
    mb = (same & (t[:, None] >= t[None, :])).astype(np.float32)
    ident = np.eye(128, dtype=np.float32)
    cm = np.ones((128, 128), np.float32)
    cm[:, ::CH] = 0.0
    rm = np.zeros((128, 128), np.float32)
    for c in range(4):
        rm[c * 32:(c + 1) * 32, c] = 1.0
    return dict(cfsf=cfsf.astype(ml_dtypes.bfloat16), masks=np.stack([mf, mb, ident, rm]), cmask=cm)


WEIGHT_KEYS = ["w_mod", "b_mod", "norm_mix", "norm_ffn", "w_in", "lb_raw", "g_norm", "w_a", "w_b", "w_out",
               "w_ff_in", "w_ff_out", "norm_final"]


_CACHE = {}


def kernel(x_prompt, x_sample, state_hgrn, c, c_ctx, w_mod, b_mod, norm_mix, norm_ffn, w_in, lb_raw,
           g_norm, w_a, w_b, w_out, w_ff_in, w_ff_out, norm_final):
    NSEG = 16
    NT = NSEG * SEG
    f32 = lambda a: np.ascontiguousarray(np.asarray(a, dtype=np.float32))
    if "nc" not in _CACHE:
        _CACHE["nc"] = build(NSEG)
    nc, _stack = _CACHE["nc"]
    w = dict(w_mod=f32(w_mod), b_mod=f32(b_mod), norm_mix=f32(norm_mix), norm_ffn=f32(norm_ffn), w_in=f32(w_in),
             lb_raw=f32(lb_raw), g_norm=f32(g_norm), w_a=f32(w_a), w_b=f32(w_b), w_out=f32(w_out),
             w_ff_in=f32(w_ff_in), w_ff_out=f32(w_ff_out), norm_final=f32(norm_final))
    cs = _consts()
    pe = _pos_embed(NT)
    dn_full = _dft_tables(NT, NT)
    dn_seg = _dft_tables(NT, SEG)
    x_prompt = f32(x_prompt)
    x_sample = f32(x_sample)
    state_hgrn = f32(state_hgrn)
    c = f32(c)
    c_ctx = f32(c_ctx)
    in_maps = []
    for core in range(8):
        m = dict(w)
        m.update(cs)
        if core < 4:
            m.update(xin=x_sample[core], pe=pe, cvec=c[core], s0=np.ascontiguousarray(state_hgrn[core]),
                     carry=np.ones((128, 1), np.float32), dn=dn_full)
        else:
            j = core - 4
            xin = np.zeros((NT, D), np.float32)
            xin[:8 * SEG] = x_prompt[8 * j:8 * j + 8].reshape(8 * SEG, D)
            m.update(xin=xin, pe=np.zeros((NT, D), np.float32), cvec=c_ctx,
                     s0=np.zeros((DEPTH, 2, H_A, 128, 128), np.float32),
                     carry=np.zeros((128, 1), np.float32), dn=dn_seg)
        in_maps.append(m)
    res = run_bass_kernel_spmd(nc, in_maps, core_ids=list(range(8)))
    r = res.results
    y_sample = np.stack([r[b]["y"] for b in range(4)], axis=0).astype(np.float32)
    y_prompt = np.concatenate([r[4 + j]["y"][:8 * SEG].reshape(8, SEG, D) for j in range(4)], axis=0).astype(np.float32)
    st = np.zeros((32, DEPTH, 2, H_A, 128, 128), np.float32)
    for j in range(4):
        s = r[4 + j]["st"]
        st[8 * j:8 * j + 8] = np.transpose(s[:, :8], (1, 0, 2, 3, 4, 5))
    return (y_prompt, y_sample, st)
```
